# Optimizing a Trainium2 kernel written in Bass

```python
import math
import jax, jax.numpy as jnp
from jax import lax
import numpy as np

D_MODEL = 2048
BATCH = 16
SEQ = 2048
DEPTH = 4

CHUNK = 64
N_MEM = 256
D_MIX = D_MODEL
MLSTM_WIDTH = D_MIX // 4
MLSTM_HEADS = 4
MLSTM_HEAD_DIM = MLSTM_WIDTH // MLSTM_HEADS
MLSTM_CONV_TAPS = 4
CCONV_WIDTH = D_MIX // 4
CCONV_TAPS = 31
ATTN_WIDTH = D_MIX - MLSTM_WIDTH - CCONV_WIDTH
ATTN_HEAD_DIM = 128
ATTN_HEADS = ATTN_WIDTH // ATTN_HEAD_DIM
LEFT_CHUNKS = 8
BAND = (LEFT_CHUNKS + 1) * CHUNK
REL_CLIP = 256
X_HEADS = 4
X_HEAD_DIM = D_MODEL // X_HEADS
D_FF = 256 * ((8 * D_MODEL // 3 + 255) // 256)
FFN_CONV_TAPS = 3
LN_EPS = 1e-5
DN_ALPHA = (2 * DEPTH) ** 0.25
DN_BETA = (8 * DEPTH) ** -0.25
IN_SIZES = (MLSTM_WIDTH, MLSTM_WIDTH, MLSTM_WIDTH, MLSTM_WIDTH, MLSTM_HEADS, MLSTM_HEADS,
            CCONV_WIDTH, CCONV_WIDTH, ATTN_WIDTH, ATTN_WIDTH, ATTN_WIDTH)
N_IN = sum(IN_SIZES)

kernel_name = 'hybrid_mlstm_conformer_chunkattn_deepnorm'


def _layer_norm(x, g, b):
    xf = x.astype(jnp.float32)
    mu = jnp.mean(xf, axis=-1, keepdims=True)
    var = jnp.mean(jnp.square(xf - mu), axis=-1, keepdims=True)
    y = (xf - mu) * lax.rsqrt(var + LN_EPS)
    return (y * g.astype(jnp.float32) + b.astype(jnp.float32)).astype(x.dtype)


def _head_norm(h, g):
    mu = jnp.mean(h, axis=-1, keepdims=True)
    var = jnp.mean(jnp.square(h - mu), axis=-1, keepdims=True)
    return (h - mu) * lax.rsqrt(var + LN_EPS) * g.astype(jnp.float32).reshape(h.shape[-2], h.shape[-1])


def _causal_dwconv(x, w, b):
    k = w.shape[0]
    y = lax.conv_general_dilated(x, w[:, None, :].astype(x.dtype), window_strides=(1,),
                                 padding=[(k - 1, 0)], dimension_numbers=('NWC', 'WIO', 'NWC'),
                                 feature_group_count=x.shape[-1])
    return y + b.astype(x.dtype)


def _mlstm(q, k, v, i_pre, f_pre):
    B, S, H, Dh = q.shape
    NC = S // CHUNK

    def to_chunks(t):
        return t.reshape(B, NC, CHUNK, H, Dh).transpose(0, 3, 1, 2, 4)

    q, k, v = to_chunks(q), to_chunks(k) * (Dh ** -0.5), to_chunks(v)
    ig = i_pre.reshape(B, NC, CHUNK, H).transpose(0, 3, 1, 2)
    lf = jax.nn.log_sigmoid(f_pre).reshape(B, NC, CHUNK, H).transpose(0, 3, 1, 2)
    b = jnp.cumsum(lf, axis=-1)
    g = b[..., -1]
    a = g[..., None] - b + ig
    a_max = jnp.max(a, axis=-1)
    w_a = jnp.exp(a - a_max[..., None])
    kv = jnp.einsum('bhcl,bhcld,bhcle->bhcde', w_a, k, v)
    nk = jnp.einsum('bhcl,bhcld->bhcd', w_a, k)

    def step(carry, xs):
        C, n, m = carry
        g_c, amax_c, kv_c, nk_c = xs
        m_new = jnp.maximum(g_c + m, amax_c)
        dec = jnp.exp(g_c + m - m_new)
        wc = jnp.exp(amax_c - m_new)
        C_new = dec[..., None, None] * C + wc[..., None, None] * kv_c
        n_new = dec[..., None] * n + wc[..., None] * nk_c
        return (C_new, n_new, m_new), (C, n, m)

    init = (jnp.zeros((B, H, Dh, Dh), jnp.float32), jnp.zeros((B, H, Dh), jnp.float32),
            jnp.zeros((B, H), jnp.float32))
    xs = (jnp.moveaxis(g, 2, 0), jnp.moveaxis(a_max, 2, 0), jnp.moveaxis(kv, 2, 0), jnp.moveaxis(nk, 2, 0))
    _, (C_prev, n_prev, m_prev) = lax.scan(step, init, xs)
    C_prev = jnp.moveaxis(C_prev, 0, 2)
    n_prev = jnp.moveaxis(n_prev, 0, 2)
    m_prev = jnp.moveaxis(m_prev, 0, 2)

    causal = jnp.tril(jnp.ones((CHUNK, CHUNK), dtype=bool))
    d_log = jnp.where(causal, b[..., :, None] - b[..., None, :] + ig[..., None, :], -jnp.inf)
    m_inter = b + m_prev[..., None]
    m_t = jnp.maximum(jnp.max(d_log, axis=-1), m_inter)
    s = jnp.einsum('bhcld,bhcsd->bhcls', q, k) * jnp.exp(d_log - m_t[..., None])
    w_inter = jnp.exp(m_inter - m_t)
    num = jnp.einsum('bhcls,bhcse->bhcle', s, v) + w_inter[..., None] * jnp.einsum('bhcld,bhcde->bhcle', q, C_prev)
    den = jnp.sum(s, axis=-1) + w_inter * jnp.einsum('bhcld,bhcd->bhcl', q, n_prev)
    h = num / jnp.maximum(jnp.abs(den), jnp.exp(-m_t))[..., None]
    return h.transpose(0, 2, 3, 1, 4).reshape(B, S, H, Dh)


def _chunk_attention(q, k, v, rel_bias):
    B, S, H, Dh = q.shape
    NC = S // CHUNK
    pad = LEFT_CHUNKS * CHUNK
    k_pad = jnp.pad(k, ((0, 0), (pad, 0), (0, 0), (0, 0)))
    v_pad = jnp.pad(v, ((0, 0), (pad, 0), (0, 0), (0, 0)))
    lq = jnp.arange(CHUNK)
    lk = jnp.arange(BAND)
    dist = (pad + lq[:, None]) - lk[None, :]
    bias = rel_bias[:, jnp.clip(dist, -REL_CLIP, REL_CLIP) + REL_CLIP].astype(jnp.float32)
    q_chunks = q.reshape(B, NC, CHUNK, H, Dh).transpose(1, 0, 2, 3, 4)
    scale = Dh ** -0.5

    def one_chunk(args):
        c, qc = args
        start = c * CHUNK
        kb = lax.dynamic_slice_in_dim(k_pad, start, BAND, axis=1)
        vb = lax.dynamic_slice_in_dim(v_pad, start, BAND, axis=1)
        s = jnp.einsum('blhd,bkhd->bhlk', qc, kb).astype(jnp.float32) * scale + bias
        valid = (start - pad + lk) >= 0
        s = jnp.where(valid, s, -jnp.inf)
        p = jax.nn.softmax(s, axis=-1).astype(vb.dtype)
        return jnp.einsum('bhlk,bkhd->blhd', p, vb)

    out = lax.map(one_chunk, (jnp.arange(NC), q_chunks))
    return out.transpose(1, 0, 2, 3, 4).reshape(B, S, H * Dh)


def _hybrid_mixer(x, w_in, mlstm_conv_w, mlstm_conv_b, mlstm_ig_b, mlstm_fg_b, mlstm_norm_g,
                  cconv_w, cconv_b, cconv_ln_g, cconv_ln_b, rel_bias, w_out):
    B, S, _ = x.shape
    f32 = jnp.float32
    offsets = []
    acc = 0
    for size in IN_SIZES[:-1]:
        acc += size
        offsets.append(acc)
    proj = x @ w_in
    mq, mk, mv, mo, mi, mf, ca, cg, aq, ak, av = jnp.split(proj, offsets, axis=-1)

    qk = jax.nn.silu(_causal_dwconv(jnp.concatenate([mq, mk], axis=-1), mlstm_conv_w, mlstm_conv_b))
    mq, mk = jnp.split(qk, 2, axis=-1)
    mh = lambda t: t.reshape(B, S, MLSTM_HEADS, MLSTM_HEAD_DIM).astype(f32)
    h_t = _mlstm(mh(mq), mh(mk), mh(mv), (mi + mlstm_ig_b).astype(f32), (mf + mlstm_fg_b).astype(f32))
    h_m = jax.nn.sigmoid(mh(mo)) * _head_norm(h_t, mlstm_norm_g)
    h_m = h_m.reshape(B, S, MLSTM_WIDTH).astype(x.dtype)

    c = ca * jax.nn.sigmoid(cg)
    c = _causal_dwconv(c, cconv_w, cconv_b)
    h_c = jax.nn.silu(_layer_norm(c, cconv_ln_g, cconv_ln_b))

    ah = lambda t: t.reshape(B, S, ATTN_HEADS, ATTN_HEAD_DIM)
    h_a = _chunk_attention(ah(aq), ah(ak), ah(av), rel_bias)

    return jnp.concatenate([h_m, h_c, h_a], axis=-1) @ w_out


def _memory_cross_attention(x, mem, xq, xkv, xo):
    B, S, _ = x.shape
    M = mem.shape[1]
    q = (x @ xq).reshape(B, S, X_HEADS, X_HEAD_DIM)
    k, v = jnp.split(mem @ xkv, 2, axis=-1)
    k = k.reshape(B, M, X_HEADS, X_HEAD_DIM)
    v = v.reshape(B, M, X_HEADS, X_HEAD_DIM)
    s = jnp.einsum('bshd,bmhd->bhsm', q, k).astype(jnp.float32) * (X_HEAD_DIM ** -0.5)
    p = jax.nn.softmax(s, axis=-1).astype(v.dtype)
    o = jnp.einsum('bhsm,bmhd->bshd', p, v).reshape(B, S, D_MODEL)
    return o @ xo


def _conv_ffn(x, w_up, conv_w, conv_b, w_down):
    gate, val = jnp.split(x @ w_up, 2, axis=-1)
    gate = _causal_dwconv(gate, conv_w, conv_b)
    return (jax.nn.gelu(gate, approximate=False) * val) @ w_down


def setup_inputs(seed: int = 0) -> dict:
    key = jax.random.key(seed)
    ks = jax.random.split(key, 32)
    f32 = jnp.float32
    L = DEPTH

    def nrm(k, shape, scale):
        return scale * jax.random.normal(k, shape, f32)

    return {
        'x': nrm(ks[0], (BATCH, SEQ, D_MODEL), 1.0),
        'mem': nrm(ks[1], (BATCH, N_MEM, D_MODEL), 1.0),
        'ln_in_g': 1.0 + nrm(ks[2], (D_MODEL,), 0.02),
        'ln_in_b': nrm(ks[3], (D_MODEL,), 0.02),
        'w_in': nrm(ks[4], (L, D_MODEL, N_IN), D_MODEL ** -0.5),
        'mlstm_conv_w': nrm(ks[5], (L, MLSTM_CONV_TAPS, 2 * MLSTM_WIDTH), MLSTM_CONV_TAPS ** -0.5),
        'mlstm_conv_b': nrm(ks[6], (L, 2 * MLSTM_WIDTH), 0.02),
        'mlstm_ig_b': nrm(ks[7], (L, MLSTM_HEADS), 0.1),
        'mlstm_fg_b': jnp.linspace(3.0, 6.0, MLSTM_HEADS, dtype=f32)[None, :] + nrm(ks[8], (L, MLSTM_HEADS), 0.1),
        'mlstm_norm_g': 1.0 + nrm(ks[9], (L, MLSTM_WIDTH), 0.02),
        'cconv_w': nrm(ks[10], (L, CCONV_TAPS, CCONV_WIDTH), CCONV_TAPS ** -0.5),
        'cconv_b': nrm(ks[11], (L, CCONV_WIDTH), 0.02),
        'cconv_ln_g': 1.0 + nrm(ks[12], (L, CCONV_WIDTH), 0.02),
        'cconv_ln_b': nrm(ks[13], (L, CCONV_WIDTH), 0.02),
        'rel_bias': nrm(ks[14], (L, ATTN_HEADS, 2 * REL_CLIP + 1), 0.2),
        'w_out': nrm(ks[15], (L, D_MIX, D_MODEL), DN_BETA * D_MIX ** -0.5),
        'ln1_g': 1.0 + nrm(ks[16], (L, D_MODEL), 0.02),
        'ln1_b': nrm(ks[17], (L, D_MODEL), 0.02),
        'xq': nrm(ks[18], (L, D_MODEL, D_MODEL), D_MODEL ** -0.5),
        'xkv': nrm(ks[19], (L, D_MODEL, 2 * D_MODEL), D_MODEL ** -0.5),
        'xo': nrm(ks[20], (L, D_MODEL, D_MODEL), DN_BETA * D_MODEL ** -0.5),
        'ln2_g': 1.0 + nrm(ks[21], (L, D_MODEL), 0.02),
        'ln2_b': nrm(ks[22], (L, D_MODEL), 0.02),
        'ffn_w_up': nrm(ks[23], (L, D_MODEL, 2 * D_FF), D_MODEL ** -0.5),
        'ffn_conv_w': nrm(ks[24], (L, FFN_CONV_TAPS, D_FF), FFN_CONV_TAPS ** -0.5),
        'ffn_conv_b': nrm(ks[25], (L, D_FF), 0.02),
        'ffn_w_down': nrm(ks[26], (L, D_FF, D_MODEL), DN_BETA * D_FF ** -0.5),
        'ln3_g': 1.0 + nrm(ks[27], (L, D_MODEL), 0.02),
        'ln3_b': nrm(ks[28], (L, D_MODEL), 0.02),
    }


def reference(x, mem, ln_in_g, ln_in_b, w_in, mlstm_conv_w, mlstm_conv_b, mlstm_ig_b, mlstm_fg_b,
              mlstm_norm_g, cconv_w, cconv_b, cconv_ln_g, cconv_ln_b, rel_bias, w_out, ln1_g, ln1_b,
              xq, xkv, xo, ln2_g, ln2_b, ffn_w_up, ffn_conv_w, ffn_conv_b, ffn_w_down, ln3_g, ln3_b):
    x = _layer_norm(x, ln_in_g, ln_in_b)
    for l in range(DEPTH):
        mix = _hybrid_mixer(x, w_in[l], mlstm_conv_w[l], mlstm_conv_b[l], mlstm_ig_b[l], mlstm_fg_b[l],
                            mlstm_norm_g[l], cconv_w[l], cconv_b[l], cconv_ln_g[l], cconv_ln_b[l],
                            rel_bias[l], w_out[l])
        x = _layer_norm(DN_ALPHA * x + mix, ln1_g[l], ln1_b[l])
        ca = _memory_cross_attention(x, mem, xq[l], xkv[l], xo[l])
        x = _layer_norm(DN_ALPHA * x + ca, ln2_g[l], ln2_b[l])
        ff = _conv_ffn(x, ffn_w_up[l], ffn_conv_w[l], ffn_conv_b[l], ffn_w_down[l])
        x = _layer_norm(DN_ALPHA * x + ff, ln3_g[l], ln3_b[l])
    return x
```

```python
import math
from contextlib import ExitStack
import numpy as np
import concourse.bass as bass
import concourse.mybir as mybir
from concourse.bass_utils import run_bass_kernel_spmd

F32 = mybir.dt.float32
BF16 = mybir.dt.bfloat16
AF = mybir.ActivationFunctionType
ALU = mybir.AluOpType
AX = mybir.AxisListType

D = 2048
KC = 16
TT = 512
DFF = 5632
NIN = 6152
ALPHA = 8.0 ** 0.25
EPS = 1e-5
BIG = 1 << 40
NCST = 740
NCPK = 516
FG = 4
FGC = 11


class View:
    __slots__ = ("root", "ap", "lo", "hi")

    def __init__(self, root, ap, lo, hi):
        self.root = root; self.ap = ap; self.lo = lo; self.hi = hi


class Tile:
    def __init__(self, ap, shape, esz, root=None, boff=0, tracked=True):
        self.ap = ap; self.shape = list(shape); self.esz = esz
        self.root = root if root is not None else self
        self.boff = boff
        if root is None:
            self.w = []; self.r = {}; self.tracked = tracked
        st = [1] * len(shape)
        for i in range(len(shape) - 2, 0, -1):
            st[i] = st[i + 1] * shape[i + 1]
        self.st = st

    def __getitem__(self, key):
        if not isinstance(key, tuple):
            key = (key,)
        lo = 0; hi = 0
        for i in range(1, len(self.shape)):
            k = key[i] if i < len(key) else slice(None)
            if isinstance(k, int):
                a = k; b = k
            else:
                a = k.start or 0
                b = (k.stop if k.stop is not None else self.shape[i]) - 1
            lo += a * self.st[i]; hi += b * self.st[i]
        if getattr(self.root, "whole", False):
            return View(self.root, self.ap[key], 0, 1 << 20)
        return View(self.root, self.ap[key], self.boff + lo * self.esz, self.boff + (hi + 1) * self.esz)

    def raw(self, ap, lo, hi):
        return View(self.root, ap, lo, hi)

    def sub(self, off, shape, f32=False):
        n = 1
        for d in shape[1:]:
            n *= d
        if f32 and "nobitcast" in DBG:
            t = _ALLOC("bc%d" % off, [128, n], F32)
            sh = list(shape)
            base = t[:, :]
            if len(shape) > 2:
                names = "abcdefg"[:len(shape) - 1]
                kw = {names[i]: shape[i + 1] for i in range(len(names) - 1)}
                base = base.rearrange("p (%s) -> p %s" % (" ".join(names), " ".join(names)), **kw)
            return Tile(base, shape, 4, root=self.root, boff=self.boff + off * 2)
        if f32:
            base = self.ap[:, off:off + 2 * n].bitcast(F32); esz = 4
        else:
            base = self.ap[:, off:off + n]; esz = 2
        if len(shape) > 2:
            names = "abcdefg"[:len(shape) - 1]
            kw = {names[i]: shape[i + 1] for i in range(len(names) - 1)}
            base = base.rearrange("p (%s) -> p %s" % (" ".join(names), " ".join(names)), **kw)
        return Tile(base, shape, esz, root=self.root, boff=self.boff + off * 2)


class Op:
    __slots__ = ("eng", "fn", "deps", "is_dma", "key", "signal", "sigval")


class Prog:
    ENG = ("pe", "act", "dve", "pool", "sp")

    def __init__(self):
        self.ops = []
        self.dma_issued = {}
        self.dma_barrier = {}

    def _add_dep(self, deps, eng, p):
        po = self.ops[p]
        if po.is_dma:
            k = po.key
            n = self.dma_issued[k]
            deps[("dma", k)] = n
            self.dma_barrier[k] = n
        else:
            if po.eng == "pe" and eng == "pe":
                return
            kk = ("eng", po.eng)
            if deps.get(kk, -1) < p:
                deps[kk] = p

    def record(self, eng, fn, reads, writes, is_dma=False, key=None):
        i = len(self.ops)
        deps = {}
        for v in reads:
            if v is None or not v.root.tracked:
                continue
            for (lo, hi, p) in v.root.w:
                if not (hi <= v.lo or lo >= v.hi):
                    self._add_dep(deps, eng, p)
        for v in writes:
            if not v.root.tracked:
                continue
            for (lo, hi, p) in v.root.w:
                if not (hi <= v.lo or lo >= v.hi):
                    self._add_dep(deps, eng, p)
            for (lo, hi, _e), p in v.root.r.items():
                if not (hi <= v.lo or lo >= v.hi):
                    self._add_dep(deps, eng, p)
        if is_dma:
            b = self.dma_barrier.get(key, 0)
            if b > 0:
                kk = ("dma", key)
                deps[kk] = max(deps.get(kk, 0), b)
            self.dma_issued[key] = self.dma_issued.get(key, 0) + 1
        for (kind, e), p in deps.items():
            if kind == "eng":
                self.ops[p].signal = True
        o = Op()
        o.eng = eng; o.fn = fn; o.deps = deps; o.is_dma = is_dma; o.key = key; o.signal = False; o.sigval = 0
        self.ops.append(o)
        for v in writes:
            if not v.root.tracked:
                continue
            r = v.root
            r.w = [e for e in r.w if not (e[0] >= v.lo and e[1] <= v.hi)]
            r.w.append((v.lo, v.hi, i))
            dead = [k for k in r.r if k[0] >= v.lo and k[1] <= v.hi]
            for k in dead:
                del r.r[k]
        for v in reads:
            if v is None or not v.root.tracked:
                continue
            tag = ("d", i) if is_dma else eng
            v.root.r[(v.lo, v.hi, tag)] = i
        return i

    def op(self, eng, method, **kw):
        reads = []; writes = []; args = {}
        for k, v in kw.items():
            if isinstance(v, View):
                (writes if k in ("out", "accum_out", "ap") else reads).append(v)
                args[k] = v.ap
            else:
                args[k] = v

        def fn(e, method=method, args=args):
            return getattr(e, method)(**args)
        i = self.record(eng, fn, reads, writes)
        self.ops[i].fn.__dict__['desc'] = (method, {k: (str(v.shape) if hasattr(v, 'shape') else v) for k, v in args.items()})
        return i

    def dma(self, q, out, in_, key):
        args = {"out": out.ap, "in_": in_.ap}

        def fn(e, args=args):
            return e.dma_start(**args)
        return self.record(q, fn, [in_], [out], is_dma=True, key=key)

    def emit(self, nc):
        cnt = {e: 0 for e in self.ENG}
        for o in self.ops:
            if (not o.is_dma) and o.signal:
                cnt[o.eng] += 1
                o.sigval = cnt[o.eng]
        with ExitStack() as es:
            esem = {e: es.enter_context(nc.semaphore("s_" + e)) for e in self.ENG}
            dsem = {k: es.enter_context(nc.semaphore("d_" + k)) for k in self.dma_issued}
            block = es.enter_context(nc.Block())
            ops = self.ops if MAXOPS is None else self.ops[:MAXOPS]
            issued = {}
            for o in ops:
                if o.is_dma:
                    issued[o.key] = issued.get(o.key, 0) + 1

            def run(name, e):
                waited = {}
                for o in ops:
                    if o.eng != name:
                        continue
                    for (kind, k), p in o.deps.items():
                        if kind == "eng":
                            sem = esem[k]; val = ops[p].sigval; sk = "e" + k
                        else:
                            sem = dsem[k]; val = 16 * p; sk = "d" + k
                        if waited.get(sk, 0) < val:
                            e.wait_ge(sem, val)
                            waited[sk] = val
                    ins = o.fn(e)
                    if o.is_dma:
                        ins.then_inc(dsem[o.key], 16)
                    elif o.signal:
                        ins.then_inc(esem[name], 1)
                if name == "sp":
                    for k, n in issued.items():
                        e.wait_ge(dsem[k], 16 * n)

            @block.tensor
            def _(e):
                run("pe", e)

            @block.scalar
            def _(e):
                run("act", e)

            @block.vector
            def _(e):
                run("dve", e)

            @block.gpsimd
            def _(e):
                run("pool", e)

            @block.sync
            def _(e):
                run("sp", e)


WSHAPES = {"w_in": (D, NIN), "w_out": (D, D), "xq": (D, D), "xkv": (D, 2 * D), "xo": (D, D),
           "w_up": (D, 2 * DFF), "w_down": (DFF, D)}


class _Stop(Exception):
    pass


NOCONV = False
MAXOPS = None
DBG = set()
_ALLOC = None


def _wtiles():
    t = {}
    t["w_in"] = [(0, KC, c0, 512) for c0 in (0, 512, 1024, 1536, 2056, 2568, 3080, 3592, 4104, 4616, 5128, 5640)]
    for k in ("w_out", "xq", "xo"):
        t[k] = [(0, KC, wt * 512, 512) for wt in range(4)]
    t["xkv"] = [(0, KC, wt * 512, 512) for wt in range(8)]
    up = []
    for half in (0, DFF):
        for g in range(FG):
            for (o, w) in ((0, 512), (512, 512), (1024, 384)):
                up.append((0, KC, half + g * FGC * 128 + o, w))
    t["w_up"] = up
    t["w_down"] = [(g * FGC, FGC, wt * 512, 512) for g in range(FG) for wt in range(4)]
    return t


WT = _wtiles()
WIDX = {k: {(r0, c0): i for i, (r0, nk, c0, nc_) in enumerate(v)} for k, v in WT.items()}


def build(NL, NSEQ, S, STOP=None):
    NT = S // TT
    NTOK = NSEQ * S
    nc = bass.Bass("TRN2", target_bir_lowering=False)
    P = Prog()
    es = ExitStack()

    def din(name, shape, dt=F32):
        return nc.dram_tensor(name, shape, dt, kind="ExternalInput").ap()

    x_d = din("x", [NTOK, D]); mem_d = din("mem", [NSEQ * 256, D])
    cst_d = din("cst", [128, NCST]); m01_d = din("m01", [128, 640]); cpk_d = din("cpk", [NL, 128, NCPK]); btab_d = din("btab", [NL, 128, 8, 640])
    w_d = {k: din(k, [NL, r, n]) for k, (r, n) in WSHAPES.items()} if "noweights" not in DBG else {}
    out_d = nc.dram_tensor("out", [NTOK, D], F32, kind="ExternalOutput").ap()
    xs_ap = nc.dram_tensor("xs", [128, KC, NTOK], F32, kind="Internal").ap()
    scr_ap = {k: nc.dram_tensor("scr_" + k, [NL, len(WT[k]), 128, KC * 512], BF16, kind="Internal").ap() for k in WSHAPES}
    scrg_ap = nc.dram_tensor("scr_wg", [NL, 128, KC * 8], BF16, kind="Internal").ap()
    NOTRK = Tile(None, [1, 1], 4, tracked=False)
    mkv_ap = nc.dram_tensor("mkv", [128, 8192], BF16, kind="Internal").ap()
    mkv_t = Tile(mkv_ap, [128, 8192], 2)
    xs_t = Tile(xs_ap, [128, KC, NTOK], 4)
    scr_t = {(k, l, t): Tile(None, [1, 1], 2) for k in WSHAPES for l in range(NL) for t in range(len(WT[k]))}
    scrg_t = {l: Tile(None, [1, 1], 2) for l in range(NL)}

    def ro(ap):
        return View(NOTRK, ap, 0, 0)

    global _ALLOC
    _ALLOC = lambda name, shape, dt: es.enter_context(nc.sbuf_tensor("sb_" + name, shape, dt))

    def sb(name, shape, dt):
        if "nosb" in DBG and name in ("xres", "xb", "actA", "kring", "vring", "wb0", "wb1", "EB", "cext", "ext", "acc"):
            shape = [128, 2, 8]
        if "nosb2" in DBG and name not in ("cst", "arena", "identb", "onesb", "cpk", "xres", "EB"):
            return Tile(None, shape, 4 if dt == F32 else 2)
        t = es.enter_context(nc.sbuf_tensor("sb_" + name, shape, dt))
        return Tile(t, shape, 4 if dt == F32 else 2)

    xres = sb("xres", [128, KC, TT], F32)
    xb = sb("xb", [128, KC, TT], BF16)
    actA = sb("actA", [128, KC, TT], BF16)
    NA = 14400
    arena = sb("arena", [128, NA], BF16)
    qkT = arena.sub(0, [128, 8, TT]); sigmo = arena.sub(4096, [128, 4, TT]); ktok = arena.sub(6144, [128, 8, TT])
    mvaug = arena.sub(10240, [128, 8, 4, 130]); aqT = arena.sub(0, [128, 8, TT])
    cacc = arena.sub(4096, [128, 4, TT], f32=True); stage = arena.sub(8192, [128, D], f32=True)
    cext = arena.sub(8192, [128, 4, 542], f32=True)
    hg = [arena.sub(0, [128, FGC, TT]), arena.sub(FGC * TT, [128, FGC, TT])]
    memT = arena.sub(0, [128, KC, 256])
    kring = sb("kring", [128, 8, 1024 if "smallsb" not in DBG else 16], BF16); vring = sb("vring", [128, 8, 1024 if "smallsb" not in DBG else 16], BF16)
    wbuf = [sb("wb0", [128, KC, 512], BF16), sb("wb1", [128, KC, 512], BF16)]
    wg = sb("wg", [128, KC, 8], BF16)
    memKT = arena.sub(4096, [128, KC, 256]); memV = arena.sub(8192, [128, 2, D])
    EB = sb("EB", [128, 8, 640], BF16)
    cst = sb("cst", [128, NCST], F32); cpk = sb("cpk", [128, NCPK], F32)
    identb = sb("identb", [128, 128], BF16); onesb = sb("onesb", [128, 128], BF16)
    ext = sb("ext", [128, 2, 515], F32); acc = sb("acc", [128, 2, TT], F32)
    etmp = arena.sub(4096, [128, 2, 640]); PTa = arena.sub(5376, [128, 2, 640]); PTx = arena.sub(0, [128, 2, 2, TT])
    ebt = arena.sub(0, [128, 640], f32=True); m01 = arena.sub(2048, [128, 640], f32=True)
    mhalo = sb("mhalo", [128, 8, 3], F32); fhalo = sb("fhalo", [128, 44, 2], F32)
    Cst = sb("Cst", [128, 516], F32)
    Cb = sb("Cb", [128, 2, 516], BF16)
    g_ig = sb("g_ig", [64, 32], F32); g_zf = sb("g_zf", [64, 32], F32); g_l1 = sb("g_l1", [64, 32], F32)
    g_nb = sb("g_nb", [64, 32], F32); g_u = sb("g_u", [64, 32], F32); g_eu = sb("g_eu", [64, 32], F32)
    g_enb = sb("g_enb", [64, 32], F32); g_eg = sb("g_eg", [128, 32], F32)
    STm = sb("STm", [64, 2, 256], BF16)
    hn = sb("hn", [64, 4, 128], BF16)
    hn2 = sb("hn2", [128, 256], BF16); vtmp = sb("vtmp", [128, 2, TT], BF16); chalo = sb("chalo", [128, 4, 30], F32)
    pnS = sb("pnS", [64, 516], F32)
    hs = sb("hs", [64, 8, 4], F32)
    ps = []
    for i in range(7):
        t = es.enter_context(nc.psum_tensor("ps%d" % i, [128, 512], F32))
        ps.append(Tile(t, [128, 512], 4))
        ps[-1].whole = True

    identf = cst[:, 0:128]; onesf = cst[:, 128:256]
    onecol = cst[:, 128:129]; epscol = cst[:, 736:737]

    st = {"wi": 0, "mb": 0, "ei": 0, "ai": 0, "cp": 0}

    def mbank():
        b = ps[st["mb"] % 4]; st["mb"] += 1
        return b

    def cp(out, in_, scale=None):
        st["cp"] += 1
        if scale is not None:
            P.op("act", "activation", out=out, in_=in_, func=AF.Copy, scale=scale)
        elif st["cp"] % 2:
            P.op("act", "activation", out=out, in_=in_, func=AF.Copy)
        else:
            P.op("dve", "tensor_copy", out=out, in_=in_)

    def mm(out, lhsT, rhs, start, stop):
        P.op("pe", "matmul", out=out, lhsT=lhsT, rhs=rhs, start=start, stop=stop)

    def convert_layer(l):
        if NOCONV:
            return
        for k in WSHAPES:
            wsrc = w_d[k][l].rearrange("(kc p) n -> p kc n", p=128)
            for t, (r0, nk, c0, ncols) in enumerate(WT[k]):
                dst = scr_ap[k][l, t].rearrange("p (kc n) -> p kc n", n=512)
                h = (nk + 1) // 2
                for j, (a, b) in enumerate(((0, h), (h, nk))):
                    P.dma("pool", out=scr_t[(k, l, t)].raw(dst[:, a:b, 0:ncols], j, j + 1),
                          in_=ro(wsrc[:, r0 + a:r0 + b, c0:c0 + ncols]), key="cv_%s%d" % (k, l))
            if k == "w_in":
                P.dma("pool", out=scrg_t[l].raw(scrg_ap[l].rearrange("p (kc n) -> p kc n", n=8), 0, 1),
                      in_=ro(wsrc[:, :, 2048:2056]), key="cv_%s%d" % (k, l))

    def wload(name, l, r0, nk, c0, ncols):
        i = st["wi"] % 2; st["wi"] += 1
        buf = wbuf[i]
        t = WIDX[name][(r0, c0)]
        assert WT[name][t] == (r0, nk, c0, ncols), (name, r0, nk, c0, ncols)
        src = scr_ap[name][l, t].rearrange("p (kc n) -> p kc n", n=512)
        P.dma("sp", out=buf[:, 0:nk, :], in_=scr_t[(name, l, t)].raw(src[:, 0:nk, :], 0, BIG), key="wb%d" % i)
        return buf

    def ln_fm(gv, bv):
        pA, pB, pM = ps[4], ps[5], ps[3]
        for c in range(KC):
            sq = acc[:, c % 2, :]
            P.op("act", "activation", out=sq, in_=xres[:, c, :], func=AF.Square)
            mm(pA[:, :], onesf, xres[:, c, :], c == 0, c == KC - 1)
            mm(pB[:, :], onesf, sq, c == 0, c == KC - 1)
        P.op("act", "activation", out=pM[:, :], in_=pA[:, :], func=AF.Copy, scale=1.0 / D)
        P.op("act", "activation", out=acc[:, 0, :], in_=pA[:, :], func=AF.Square, scale=1.0 / D)
        P.op("dve", "scalar_tensor_tensor", out=pA[:, :], in0=pB[:, :], scalar=1.0 / D, in1=acc[:, 0, :],
             op0=ALU.mult, op1=ALU.subtract)
        P.op("act", "activation", out=pA[:, :], in_=pA[:, :], func=AF.Ln, bias=epscol)
        P.op("act", "activation", out=pB[:, :], in_=pA[:, :], func=AF.Exp, scale=-0.5)
        for c in range(KC):
            xc = xres[:, c, :]
            P.op("dve", "tensor_tensor", out=xc, in0=xc, in1=pM[:, :], op=ALU.subtract)
            P.op("dve", "tensor_tensor", out=xc, in0=xc, in1=pB[:, :], op=ALU.mult)
            P.op("act", "activation", out=xc, in_=xc, func=AF.Identity, scale=gv(c), bias=bv(c))
            P.op("act", "activation", out=xb[:, c, :], in_=xc, func=AF.Copy)

    def resid(cbase, first=True):
        def f(oc, p):
            xc = xres[:, cbase + oc, :]
            if first:
                P.op("dve", "scalar_tensor_tensor", out=xc, in0=xc, scalar=ALPHA, in1=p[:, :], op0=ALU.mult, op1=ALU.add)
            else:
                P.op("dve", "tensor_tensor", out=xc, in0=xc, in1=p[:, :], op=ALU.add)
        return f

    def fm_proj(W, nk, noc, rhs_fn, consume, ncols=TT):
        for oc in range(noc):
            p = mbank()
            for kc in range(nk):
                mm(p[:, 0:ncols], W[:, kc, oc * 128:(oc + 1) * 128], rhs_fn(kc), kc == 0, kc == nk - 1)
            consume(oc, p)

    xbk = lambda kc: xb[:, kc, :]

    def chk(label):
        if STOP == label:
            raise _Stop()

    def store_out(tok0):
        for blk in range(4):
            for c in range(KC):
                pt = ps[4 + c % 2]
                P.op("pe", "transpose", out=pt[:, 0:128], in_=xres[:, c, blk * 128:(blk + 1) * 128], identity=identf)
                cp(stage[:, c * 128:(c + 1) * 128], pt[:, 0:128])
            P.dma("sp", out=ro(out_d[tok0 + blk * 128:tok0 + (blk + 1) * 128, :]), in_=stage[:, :], key="ost")

    P.dma("sp", out=cst[:, :], in_=ro(cst_d[:, :]), key="cst")
    P.op("act", "activation", out=identb[:, :], in_=identf, func=AF.Copy)
    P.op("act", "activation", out=onesb[:, :], in_=onesf, func=AF.Copy)
    convert_layer(0)

    try:
        for l in range(NL):
            if l + 1 < NL:
                convert_layer(l + 1)
            P.dma("sp", out=cpk[:, :], in_=ro(cpk_d[l]), key="cpk")
            P.dma("sp", out=m01[:, :], in_=ro(m01_d[:, :]), key="m01")
            for h in range(8):
                P.dma("sp", out=ebt[:, :], in_=ro(btab_d[l, :, h, :]), key="ebt")
                P.op("act", "activation", out=ebt[:, :], in_=ebt[:, :], func=AF.Exp)
                P.op("dve", "tensor_tensor", out=EB[:, h, :], in0=ebt[:, :], in1=m01[:, :], op=ALU.mult)
            col = lambda base: (lambda c: cpk[:, base + c:base + c + 1])
            chk('setup')

            for b in range(NSEQ):
                for mb in range(2):
                    P.dma("sp", out=stage[:, :], in_=ro(mem_d[b * 256 + mb * 128:b * 256 + (mb + 1) * 128, :]), key="stage")
                    for c in range(KC):
                        pt = ps[4 + c % 2]
                        P.op("pe", "transpose", out=pt[:, 0:128], in_=stage[:, c * 128:(c + 1) * 128], identity=identf)
                        cp(memT[:, c, mb * 128:(mb + 1) * 128], pt[:, 0:128])
                for wt in range(4):
                    W = wload("xkv", l, 0, KC, wt * 512, 512)
                    fm_proj(W, KC, 4, lambda kc: memT[:, kc, :],
                            lambda oc, p, wt=wt: cp(memKT[:, wt * 4 + oc, :], p[:, 0:256]), ncols=256)
                for wt in range(4):
                    W = wload("xkv", l, 0, KC, D + wt * 512, 512)
                    for mb in range(2):
                        p = mbank()
                        for kc in range(KC):
                            mm(p[:, :], memT[:, kc, mb * 128:(mb + 1) * 128], W[:, kc, :], kc == 0, kc == KC - 1)
                        cp(memV[:, mb, wt * 512:(wt + 1) * 512], p[:, :])
                P.dma("sp", out=mkv_t[:, 0:8192], in_=arena[:, 4096:12288], key="mkvst")
                chk('memkv')
                P.op("dve", "memset", ap=mhalo[:, :, :], constant=0.0)
                P.op("dve", "memset", ap=fhalo[:, :, :], constant=0.0)
                P.op("dve", "memset", ap=chalo[:, :, :], constant=0.0)
                P.op("dve", "memset", ap=Cst[:, :], constant=0.0)
                P.op("dve", "memset", ap=Cb[:, 0, :], constant=0.0)
                cbi = 0

                for t in range(NT):
                    tok0 = b * S + t * TT
                    if l == 0:
                        for blk in range(4):
                            P.dma("sp", out=stage[:, :], in_=ro(x_d[tok0 + blk * 128:tok0 + (blk + 1) * 128, :]), key="stage")
                            for c in range(KC):
                                pt = ps[4 + c % 2]
                                P.op("pe", "transpose", out=pt[:, 0:128], in_=stage[:, c * 128:(c + 1) * 128], identity=identf)
                                cp(xres[:, c, blk * 128:(blk + 1) * 128], pt[:, 0:128])
                        ln_fm(lambda c: cst[:, 704 + c:705 + c], lambda c: cst[:, 720 + c:721 + c])
                        chk('ln_in')
                    else:
                        for (a, bb) in ((0, 8), (8, 16)):
                            P.dma("sp", out=xres[:, a:bb, :], in_=xs_t.raw(xs_ap[:, a:bb, tok0:tok0 + TT], (tok0 // TT) * 2 + a // 8, (tok0 // TT) * 2 + a // 8 + 1), key="xres")
                        for c in range(KC):
                            cp(xb[:, c, :], xres[:, c, :])

                    P.dma("sp", out=wg[:, :, :], in_=scrg_t[l].raw(scrg_ap[l].rearrange("p (kc n) -> p kc n", n=8), 0, BIG), key="wg")

                    def conv_m(ch, p, outv):
                        e = st["ei"] % 2; st["ei"] += 1
                        a = st["ai"] % 2; st["ai"] += 1
                        P.op("act", "activation", out=ext[:, e, 0:3], in_=mhalo[:, ch, :], func=AF.Copy)
                        P.op("act", "activation", out=ext[:, e, 3:515], in_=p[:, :], func=AF.Copy)
                        P.op("act", "activation", out=mhalo[:, ch, :], in_=ext[:, e, 512:515], func=AF.Copy)
                        av = acc[:, a, :]
                        P.op("dve", "tensor_scalar", out=av, in0=ext[:, e, 0:512], scalar1=cpk[:, 96 + ch * 4:97 + ch * 4],
                             scalar2=cpk[:, 128 + ch:129 + ch], op0=ALU.mult, op1=ALU.add)
                        for j in range(1, 4):
                            P.op("dve", "scalar_tensor_tensor", out=av, in0=ext[:, e, j:j + 512],
                                 scalar=cpk[:, 96 + ch * 4 + j:97 + ch * 4 + j], in1=av, op0=ALU.mult, op1=ALU.add)
                        P.op("act", "activation", out=outv, in_=av, func=AF.Silu)

                    W = wload("w_in", l, 0, KC, 0, 512)
                    fm_proj(W, KC, 4, xbk, lambda oc, p: conv_m(oc, p, qkT[:, oc, :]))
                    chk('m_q')
                    W = wload("w_in", l, 0, KC, 512, 512)
                    fm_proj(W, KC, 4, xbk, lambda oc, p: conv_m(4 + oc, p, qkT[:, 4 + oc, :]))
                    chk('m_k')
                    for c in range(8):
                        pt = ps[6 if c % 2 == 0 else 3]
                        for h in range(4):
                            mm(pt[0:64, h * 128:(h + 1) * 128], qkT[:, 4 + h, c * 64:(c + 1) * 64], identb[:, :], True, True)
                        cp(ktok[0:64, c, :], pt[0:64, :], scale=128.0 ** -0.5)
                    chk('m_kt')
                    pgf = ps[4]
                    for kc in range(KC):
                        mm(pgf[0:8, :], wg[:, kc, 0:8], xb[:, kc, :], kc == 0, kc == KC - 1)
                    P.op("act", "activation", out=acc[0:8, 0, :], in_=pgf[0:8, :], func=AF.Copy)
                    pg = ps[5]
                    for c in range(8):
                        mm(pg[0:64, c * 4:(c + 1) * 4], acc[0:8, 0, c * 64:(c + 1) * 64], cst[0:8, 0:4], True, True)
                        mm(pg[0:64, 32 + c * 4:32 + (c + 1) * 4], acc[0:8, 0, c * 64:(c + 1) * 64], cst[0:8, 4:8], True, True)
                    P.op("dve", "tensor_tensor", out=g_ig[:, :], in0=pg[0:64, 0:32], in1=cpk[0:64, 452:484], op=ALU.add)
                    P.op("dve", "tensor_tensor", out=g_zf[:, :], in0=pg[0:64, 32:64], in1=cpk[0:64, 484:516], op=ALU.add)
                    P.op("act", "activation", out=g_zf[:, :], in_=g_zf[:, :], func=AF.Exp, scale=-1.0)
                    P.op("act", "activation", out=g_l1[:, :], in_=g_zf[:, :], func=AF.Ln, bias=cst[0:64, 128:129])
                    mm(ps[4][0:64, 0:32], cst[0:64, 256:320], g_l1[:, :], True, True)
                    P.op("dve", "tensor_copy", out=g_nb[:, :], in_=ps[4][0:64, 0:32])
                    P.op("dve", "tensor_tensor", out=g_u[:, :], in0=g_ig[:, :], in1=g_nb[:, :], op=ALU.add)
                    P.op("act", "activation", out=g_eu[:, :], in_=g_u[:, :], func=AF.Exp)
                    P.op("act", "activation", out=g_enb[:, :], in_=g_nb[:, :], func=AF.Exp)
                    mm(ps[4][:, 32:64], cst[0:64, 576:704], g_nb[:, :], True, True)
                    P.op("act", "activation", out=g_eg[:, :], in_=ps[4][:, 32:64], func=AF.Exp, scale=-1.0)
                    chk('m_g')
                    W = wload("w_in", l, 0, KC, 1024, 512)
                    for c in range(8):
                        p = ps[3]
                        for kc in range(KC):
                            mm(p[0:64, :], xb[:, kc, c * 64:(c + 1) * 64], W[:, kc, :], kc == 0, kc == KC - 1)
                        for h in range(4):
                            P.op("dve", "tensor_scalar", out=mvaug[0:64, c, h, 0:128], in0=p[0:64, h * 128:(h + 1) * 128],
                                 scalar1=g_eu[:, c * 4 + h:c * 4 + h + 1], scalar2=None, op0=ALU.mult)
                        P.op("act", "activation", out=mvaug[0:64, c, :, 128], in_=g_eu[:, c * 4:c * 4 + 4], func=AF.Copy)
                    chk('m_v')
                    W = wload("w_in", l, 0, KC, 1536, 512)
                    fm_proj(W, KC, 4, xbk, lambda oc, p: P.op("act", "activation", out=sigmo[:, oc, :], in_=p[:, :], func=AF.Sigmoid))
                    chk('m_o')
                    for c in range(8):
                        par = c % 2
                        cs = slice(c * 64, (c + 1) * 64)
                        pkq = ps[4]
                        for h in range(4):
                            mm(pkq[0:64, par * 256 + h * 64:par * 256 + (h + 1) * 64], qkT[:, 4 + h, cs], qkT[:, h, cs], True, True)
                        P.op("dve", "tensor_tensor", out=STm[:, par, :], in0=pkq[0:64, par * 256:(par + 1) * 256],
                             in1=cst[0:64, 320:576], op=ALU.mult)
                        def pnv(h, a, b):
                            return (ps[5] if h < 2 else ps[3])[0:64, (h % 2) * 129 + a:(h % 2) * 129 + b]

                        def pkvv(h):
                            return (ps[1] if h < 2 else ps[2])[:, (h % 2) * 129:(h % 2) * 129 + 129]
                        for h in range(4):
                            mm(pnv(h, 0, 129), STm[:, par, h * 64:(h + 1) * 64], mvaug[0:64, c, h, 0:129], True, False)
                            mm(pnv(h, 0, 129), qkT[:, h, cs], Cb[:, cbi, h * 129:(h + 1) * 129], False, True)
                        for h in range(4):
                            mm(pkvv(h), ktok[0:64, c, h * 128:(h + 1) * 128], mvaug[0:64, c, h, 0:129], True, True)
                        P.op("act", "activation", out=pnS[:, 0:258], in_=ps[5][0:64, 0:258], func=AF.Copy)
                        P.op("dve", "tensor_copy", out=pnS[:, 258:516], in_=ps[3][0:64, 0:258])
                        denv = pnS.raw(pnS.ap[:, :].rearrange("p (h e) -> p h e", h=4)[:, :, 128], 0, 2064)
                        P.op("dve", "scalar_tensor_tensor", out=hs[:, 0, :], in0=denv, scalar=-1.0, in1=denv, op0=ALU.mult, op1=ALU.max)
                        P.op("dve", "tensor_tensor", out=hs[:, 0, :], in0=hs[:, 0, :], in1=g_enb[:, c * 4:c * 4 + 4], op=ALU.max)
                        P.op("dve", "reciprocal", out=hs[:, 0, :], in_=hs[:, 0, :])
                        for h in range(4):
                            P.op("dve", "tensor_scalar", out=acc[0:64, 1, h * 128:(h + 1) * 128], in0=pnS[:, h * 129:h * 129 + 128],
                                 scalar1=hs[:, 0, h:h + 1], scalar2=None, op0=ALU.mult)
                        P.op("dve", "tensor_reduce", out=hs[:, 1, :], in_=acc.raw(acc.ap[0:64, 1, :].rearrange("p (h e) -> p h e", h=4), 2048, 4096), axis=AX.X, op=ALU.add)
                        P.op("act", "activation", out=acc[0:64, 0, :], in_=acc[0:64, 1, :], func=AF.Square)
                        P.op("dve", "tensor_reduce", out=hs[:, 2, :], in_=acc.raw(acc.ap[0:64, 0, :].rearrange("p (h e) -> p h e", h=4), 0, 2048), axis=AX.X, op=ALU.add)
                        P.op("dve", "tensor_scalar", out=hs[:, 3, :], in0=hs[:, 1, :], scalar1=1.0 / 128, scalar2=None, op0=ALU.mult)
                        P.op("dve", "tensor_tensor", out=hs[:, 6, :], in0=hs[:, 3, :], in1=hs[:, 3, :], op=ALU.mult)
                        P.op("dve", "scalar_tensor_tensor", out=hs[:, 4, :], in0=hs[:, 2, :], scalar=1.0 / 128, in1=hs[:, 6, :],
                             op0=ALU.mult, op1=ALU.subtract)
                        P.op("act", "activation", out=hs[:, 4, :], in_=hs[:, 4, :], func=AF.Ln, bias=cst[0:64, 736:737])
                        P.op("act", "activation", out=hs[:, 5, :], in_=hs[:, 4, :], func=AF.Exp, scale=-0.5)
                        for h in range(4):
                            P.op("dve", "tensor_scalar", out=hn[:, h, :], in0=acc[0:64, 1, h * 128:(h + 1) * 128], scalar1=hs[:, 3, h:h + 1],
                                 scalar2=hs[:, 5, h:h + 1], op0=ALU.subtract, op1=ALU.mult)
                        half = par * 256
                        for h in range(4):
                            mm(ps[6][:, half + h * 64:half + (h + 1) * 64], hn[:, h, :], identb[0:64, 0:64], True, True)
                        for h in range(4):
                            P.op("act", "activation", out=hn2[:, h * 64:(h + 1) * 64], in_=ps[6][:, half + h * 64:half + (h + 1) * 64],
                                 func=AF.Identity, scale=cpk[:, 136 + h:137 + h])
                            P.op("dve", "tensor_tensor", out=actA[:, h, cs], in0=hn2[:, h * 64:(h + 1) * 64], in1=sigmo[:, h, cs], op=ALU.mult)
                        P.op("dve", "tensor_tensor", out=Cst[:, 0:258], in0=Cst[:, 0:258], in1=ps[1][:, 0:258], op=ALU.add)
                        P.op("dve", "tensor_tensor", out=Cst[:, 258:516], in0=Cst[:, 258:516], in1=ps[2][:, 0:258], op=ALU.add)
                        for h in range(4):
                            P.op("dve", "tensor_scalar", out=Cst[:, h * 129:(h + 1) * 129], in0=Cst[:, h * 129:(h + 1) * 129],
                                 scalar1=g_eg[:, c * 4 + h:c * 4 + h + 1], scalar2=None, op0=ALU.mult)
                        cbi ^= 1
                        P.op("act", "activation", out=Cb[:, cbi, :], in_=Cst[:, :], func=AF.Copy)
                        chk('m_c%d' % c)

                    chk('mlstm')
                    W = wload("w_in", l, 0, KC, 2056, 512)
                    fm_proj(W, KC, 4, xbk, lambda oc, p: cp(cacc[:, oc, :], p[:, :]))
                    W = wload("w_in", l, 0, KC, 2568, 512)

                    def glu(oc, p):
                        a = st["ai"] % 2; st["ai"] += 1
                        P.op("act", "activation", out=acc[:, a, :], in_=p[:, :], func=AF.Sigmoid)
                        P.op("dve", "tensor_tensor", out=cext[:, oc, 30:542], in0=cacc[:, oc, :], in1=acc[:, a, :], op=ALU.mult)
                    fm_proj(W, KC, 4, xbk, glu)
                    for ch in range(4):
                        cv = cacc[:, ch, :]
                        P.op("act", "activation", out=cext[:, ch, 0:30], in_=chalo[:, ch, :], func=AF.Copy)
                        P.op("dve", "tensor_scalar", out=cv, in0=cext[:, ch, 0:512], scalar1=cpk[:, 140 + ch * 31:141 + ch * 31],
                             scalar2=cpk[:, 264 + ch:265 + ch], op0=ALU.mult, op1=ALU.add)
                        for j in range(1, 31):
                            P.op("dve", "scalar_tensor_tensor", out=cv, in0=cext[:, ch, j:j + 512],
                                 scalar=cpk[:, 140 + ch * 31 + j:141 + ch * 31 + j], in1=cv, op0=ALU.mult, op1=ALU.add)
                        P.op("act", "activation", out=chalo[:, ch, :], in_=cext[:, ch, 512:542], func=AF.Copy)
                    pA, pB, pM = ps[4], ps[5], ps[3]
                    for ch in range(4):
                        sq = acc[:, ch % 2, :]
                        P.op("act", "activation", out=sq, in_=cacc[:, ch, :], func=AF.Square)
                        mm(pA[:, :], onesf, cacc[:, ch, :], ch == 0, ch == 3)
                        mm(pB[:, :], onesf, sq, ch == 0, ch == 3)
                    P.op("act", "activation", out=pM[:, :], in_=pA[:, :], func=AF.Copy, scale=1.0 / 512)
                    P.op("act", "activation", out=acc[:, 0, :], in_=pA[:, :], func=AF.Square, scale=1.0 / 512)
                    P.op("dve", "scalar_tensor_tensor", out=pA[:, :], in0=pB[:, :], scalar=1.0 / 512, in1=acc[:, 0, :],
                         op0=ALU.mult, op1=ALU.subtract)
                    P.op("act", "activation", out=pA[:, :], in_=pA[:, :], func=AF.Ln, bias=epscol)
                    P.op("act", "activation", out=pB[:, :], in_=pA[:, :], func=AF.Exp, scale=-0.5)
                    for ch in range(4):
                        cv = cacc[:, ch, :]
                        P.op("dve", "tensor_tensor", out=cv, in0=cv, in1=pM[:, :], op=ALU.subtract)
                        P.op("dve", "tensor_tensor", out=cv, in0=cv, in1=pB[:, :], op=ALU.mult)
                        P.op("act", "activation", out=actA[:, 4 + ch, :], in_=cv, func=AF.Silu,
                             scale=cpk[:, 268 + ch:269 + ch], bias=cpk[:, 272 + ch:273 + ch])

                    chk('cconv')
                    for wt in range(2):
                        W = wload("w_in", l, 0, KC, 3080 + wt * 512, 512)
                        fm_proj(W, KC, 4, xbk, lambda oc, p, wt=wt: cp(aqT[:, wt * 4 + oc, :], p[:, :], scale=128.0 ** -0.5))
                    slot0 = (t * 4) % 8
                    for wt in range(2):
                        W = wload("w_in", l, 0, KC, 4104 + wt * 512, 512)
                        fm_proj(W, KC, 4, xbk, lambda oc, p, wt=wt: cp(kring[:, wt * 4 + oc, slot0 * 128:slot0 * 128 + 512], p[:, :]))
                    for wt in range(2):
                        W = wload("w_in", l, 0, KC, 5128 + wt * 512, 512)
                        for blk in range(4):
                            p = mbank()
                            for kc in range(KC):
                                mm(p[:, :], xb[:, kc, blk * 128:(blk + 1) * 128], W[:, kc, :], kc == 0, kc == KC - 1)
                            cp(vring[:, slot0 + blk, wt * 512:(wt + 1) * 512], p[:, :])
                    sk = 0
                    for h in range(8):
                        po = ps[0 + 2 * (h % 2)]; pdn = ps[1 + 2 * (h % 2)]
                        for j in range(4):
                            gq = t * 4 + j
                            ivs = [i for i in range(5) if gq - 4 + i >= 0]
                            k2 = sk % 2; sk += 1
                            pa = ps[4]; pb5 = ps[5]
                            qv = aqT[:, h, j * 128:(j + 1) * 128]

                            def sview(i):
                                return pa[:, i * 128:(i + 1) * 128] if i < 4 else pb5[:, k2 * 128:(k2 + 1) * 128]
                            for i in ivs:
                                sl = (gq - 4 + i) % 8
                                mm(sview(i), kring[:, h, sl * 128:(sl + 1) * 128], qv, True, True)
                            i0 = ivs[0]
                            if i0 < 4:
                                P.op("act", "activation", out=etmp[:, k2, i0 * 128:512], in_=pa[:, i0 * 128:512], func=AF.Exp)
                            P.op("act", "activation", out=etmp[:, k2, 512:640], in_=pb5[:, k2 * 128:(k2 + 1) * 128], func=AF.Exp)
                            P.op("dve", "tensor_tensor", out=PTa[:, k2, i0 * 128:640], in0=etmp[:, k2, i0 * 128:640],
                                 in1=EB[:, h, i0 * 128:640], op=ALU.mult)
                            for n, i in enumerate(ivs):
                                sl = (gq - 4 + i) % 8
                                mm(po[:, j * 128:(j + 1) * 128], vring[:, sl, h * 128:(h + 1) * 128], PTa[:, k2, i * 128:(i + 1) * 128],
                                   n == 0, n == len(ivs) - 1)
                            for n, i in enumerate(ivs):
                                mm(pdn[:, j * 128:(j + 1) * 128], onesb[:, :], PTa[:, k2, i * 128:(i + 1) * 128], n == 0, n == len(ivs) - 1)
                        P.op("dve", "reciprocal", out=acc[:, 1, :], in_=pdn[:, :])
                        P.op("dve", "tensor_tensor", out=actA[:, 8 + h, :], in0=po[:, :], in1=acc[:, 1, :], op=ALU.mult)

                    chk('attn')
                    for wt in range(4):
                        W = wload("w_out", l, 0, KC, wt * 512, 512)
                        fm_proj(W, KC, 4, lambda kc: actA[:, kc, :], resid(wt * 4))
                    ln_fm(col(0), col(16))

                    chk('ln1')
                    P.dma("sp", out=arena[:, 4096:12288], in_=mkv_t[:, 0:8192], key="mkv")
                    for wt in range(4):
                        W = wload("xq", l, 0, KC, wt * 512, 512)
                        fm_proj(W, KC, 4, xbk, lambda oc, p, wt=wt: cp(actA[:, wt * 4 + oc, :], p[:, :], scale=512.0 ** -0.5))
                    for hx in range(4):
                        k2 = hx % 2
                        for mb in range(2):
                            p = ps[4 + mb]
                            for dc in range(4):
                                mm(p[:, :], memKT[:, hx * 4 + dc, mb * 128:(mb + 1) * 128], actA[:, hx * 4 + dc, :], dc == 0, dc == 3)
                            P.op("act", "activation", out=PTx[:, k2, mb, :], in_=p[:, :], func=AF.Exp)
                        pdn = ps[3]
                        for mb in range(2):
                            mm(pdn[:, :], onesb[:, :], PTx[:, k2, mb, :], mb == 0, mb == 1)
                        P.op("dve", "reciprocal", out=acc[:, 1, :], in_=pdn[:, :])
                        for fc in range(4):
                            p = mbank()
                            for mb in range(2):
                                mm(p[:, :], memV[:, mb, (hx * 4 + fc) * 128:(hx * 4 + fc + 1) * 128], PTx[:, k2, mb, :], mb == 0, mb == 1)
                            P.op("dve", "tensor_tensor", out=xb[:, hx * 4 + fc, :], in0=p[:, :], in1=acc[:, 1, :], op=ALU.mult)
                    for wt in range(4):
                        W = wload("xo", l, 0, KC, wt * 512, 512)
                        fm_proj(W, KC, 4, xbk, resid(wt * 4))
                    ln_fm(col(32), col(48))

                    chk('ln2')
                    for g in range(FG):
                        hgv = hg[g % 2]
                        widths = (512, 512, 384)
                        c0 = g * FGC * 128
                        jj = 0
                        for wcols in widths:
                            W = wload("w_up", l, 0, KC, c0 + jj * 128, wcols)

                            def gate_c(oc, p, jj=jj):
                                j = g * FGC + jj + oc
                                e = st["ei"] % 2; st["ei"] += 1
                                a = st["ai"] % 2; st["ai"] += 1
                                P.op("act", "activation", out=ext[:, e, 0:2], in_=fhalo[:, j, :], func=AF.Copy)
                                P.op("act", "activation", out=ext[:, e, 2:514], in_=p[:, :], func=AF.Copy)
                                P.op("act", "activation", out=fhalo[:, j, :], in_=ext[:, e, 512:514], func=AF.Copy)
                                av = acc[:, a, :]
                                P.op("dve", "tensor_scalar", out=av, in0=ext[:, e, 0:512], scalar1=cpk[:, 276 + j * 3:277 + j * 3],
                                     scalar2=cpk[:, 408 + j:409 + j], op0=ALU.mult, op1=ALU.add)
                                for q in range(1, 3):
                                    P.op("dve", "scalar_tensor_tensor", out=av, in0=ext[:, e, q:q + 512],
                                         scalar=cpk[:, 276 + j * 3 + q:277 + j * 3 + q], in1=av, op0=ALU.mult, op1=ALU.add)
                                P.op("act", "activation", out=hgv[:, jj + oc, :], in_=av, func=AF.Gelu)
                            fm_proj(W, KC, wcols // 128, xbk, gate_c)
                            jj += wcols // 128
                        jj = 0
                        for wcols in widths:
                            W = wload("w_up", l, 0, KC, DFF + c0 + jj * 128, wcols)
                            def val_c(oc, p, jj=jj):
                                a = st["ai"] % 2; st["ai"] += 1
                                P.op("act", "activation", out=vtmp[:, a, :], in_=p[:, :], func=AF.Copy)
                                P.op("dve", "tensor_tensor", out=hgv[:, jj + oc, :], in0=vtmp[:, a, :], in1=hgv[:, jj + oc, :], op=ALU.mult)
                            fm_proj(W, KC, wcols // 128, xbk, val_c)
                            jj += wcols // 128
                        for wt in range(4):
                            W = wload("w_down", l, g * FGC, FGC, wt * 512, 512)
                            fm_proj(W, FGC, 4, lambda kc: hgv[:, kc, :], resid(wt * 4, first=(g == 0)))
                    ln_fm(col(64), col(80))

                    if l == NL - 1:
                        store_out(tok0)
                    else:
                        for (a, bb) in ((0, 8), (8, 16)):
                            P.dma("sp", out=xs_t.raw(xs_ap[:, a:bb, tok0:tok0 + TT], (tok0 // TT) * 2 + a // 8, (tok0 // TT) * 2 + a // 8 + 1), in_=xres[:, a:bb, :], key="xst")

    except _Stop:
        if STOP == 'm_g':
            P.op('dve', 'tensor_copy', out=xres[0:64, 0, 0:32], in_=g_eu[:, :])
            P.op('dve', 'tensor_copy', out=xres[0:64, 0, 32:64], in_=g_enb[:, :])
            P.op('dve', 'tensor_copy', out=xres[:, 0, 64:96], in_=g_eg[:, :])
            P.op('dve', 'tensor_copy', out=xres[0:64, 0, 96:128], in_=g_nb[:, :])
        if STOP == 'm_k':
            for c in range(8):
                cp(xres[:, c, :], qkT[:, c, :])
        if STOP in ('mlstm', 'cconv', 'attn'):
            for c in range(KC):
                cp(xres[:, c, :], actA[:, c, :])
        store_out(0)
    P.emit(nc)
    es.close()
    return nc


def _consts():
    c = np.zeros((128, NCST), np.float32)
    c[:, 0:128] = np.eye(128, dtype=np.float32)
    c[:, 128:256] = 1.0
    tri = np.triu(np.ones((64, 64), np.float32))
    c[0:64, 256:320] = tri
    c[0:64, 320:576] = np.tile(tri * np.float32(128.0 ** -0.5), (1, 4))
    c[63, 576:704] = 1.0
    m = np.ones((128, 5, 128), np.float32)
    m[64:, 4, :64] = 0.0
    m[:64, 0, 64:] = 0.0
    c[:, 736] = EPS
    return c, np.ascontiguousarray(m.reshape(128, 640))


def _colmajor(v):
    return np.ascontiguousarray(np.asarray(v, np.float32).reshape(-1, 128).T)


def _prep(inp, NL):
    cst, m01 = _consts()
    cst[:, 704:720] = _colmajor(inp["ln_in_g"]); cst[:, 720:736] = _colmajor(inp["ln_in_b"])
    cpk = np.zeros((NL, 128, NCPK), np.float32)
    r = np.arange(128)[:, None, None]; ii = np.arange(5)[None, :, None]; cq = np.arange(128)[None, None, :]
    dist = 128 * (4 - ii) + cq - r
    idx = np.clip(dist, -256, 256) + 256
    btab = np.zeros((NL, 128, 8, 640), np.float32)
    for l in range(NL):
        for k, off in (("ln1_g", 0), ("ln1_b", 16), ("ln2_g", 32), ("ln2_b", 48), ("ln3_g", 64), ("ln3_b", 80)):
            cpk[l, :, off:off + 16] = _colmajor(inp[k][l])
        w = np.asarray(inp["mlstm_conv_w"][l], np.float32)
        cpk[l, :, 96:128] = w.reshape(4, 8, 128).transpose(2, 1, 0).reshape(128, 32)
        cpk[l, :, 128:136] = _colmajor(inp["mlstm_conv_b"][l])
        cpk[l, :, 136:140] = _colmajor(inp["mlstm_norm_g"][l])
        w = np.asarray(inp["cconv_w"][l], np.float32)
        cpk[l, :, 140:264] = w.reshape(31, 4, 128).transpose(2, 1, 0).reshape(128, 124)
        cpk[l, :, 264:268] = _colmajor(inp["cconv_b"][l])
        cpk[l, :, 268:272] = _colmajor(inp["cconv_ln_g"][l])
        cpk[l, :, 272:276] = _colmajor(inp["cconv_ln_b"][l])
        w = np.asarray(inp["ffn_conv_w"][l], np.float32)
        cpk[l, :, 276:408] = w.reshape(3, 44, 128).transpose(2, 1, 0).reshape(128, 132)
        cpk[l, :, 408:452] = _colmajor(inp["ffn_conv_b"][l])
        cpk[l, :, 452:484] = np.tile(np.asarray(inp["mlstm_ig_b"][l], np.float32), 8)[None, :]
        cpk[l, :, 484:516] = np.tile(np.asarray(inp["mlstm_fg_b"][l], np.float32), 8)[None, :]
        rb = np.asarray(inp["rel_bias"][l], np.float32)
        btab[l] = rb[:, idx].transpose(1, 0, 2, 3).reshape(128, 8, 640)
    return cst, m01, cpk, btab


_NC_CACHE = {}


def run_model(inp, NL, NSEQ, S, ncores):
    key = (NL, NSEQ, S)
    if key not in _NC_CACHE:
        _NC_CACHE[key] = build(NL, NSEQ, S)
    nc = _NC_CACHE[key]
    cst, m01, cpk, btab = _prep(inp, NL)
    x = np.asarray(inp["x"], np.float32); mem = np.asarray(inp["mem"], np.float32)
    wts = {k: np.ascontiguousarray(np.asarray(inp["ffn_" + k if k in ("w_up", "w_down") else k], np.float32)[:NL])
           for k in WSHAPES}
    in_maps = []
    for i in range(ncores):
        m = {"x": np.ascontiguousarray(x[i * NSEQ:(i + 1) * NSEQ].reshape(NSEQ * S, D)),
             "mem": np.ascontiguousarray(mem[i * NSEQ:(i + 1) * NSEQ].reshape(NSEQ * 256, D)),
             "cst": cst, "m01": m01, "cpk": cpk, "btab": btab}
        if "noweights" not in DBG:
            m.update(wts)
        in_maps.append(m)
    res = run_bass_kernel_spmd(nc, in_maps, core_ids=list(range(ncores)))
    outs = [np.asarray(r["out"]).reshape(NSEQ, S, D) for r in res.results]
    return np.concatenate(outs, axis=0).astype(np.float32)


def kernel(**inputs):
    return run_model(inputs, 4, 2, 2048, 8)
```

```python
import math
from contextlib import ExitStack
import numpy as np
import concourse.bass as bass
import concourse.mybir as mybir
from concourse.bass_utils import run_bass_kernel_spmd

F32 = mybir.dt.float32
BF16 = mybir.dt.bfloat16
AF = mybir.ActivationFunctionType
ALU = mybir.AluOpType
AX = mybir.AxisListType

D = 2048
KC = 16
TT = 512
DFF = 5632
NIN = 6152
ALPHA = 8.0 ** 0.25
EPS = 1e-5
BIG = 1 << 40
NCST = 740
NCPK = 516
FG = 4
FGC = 11


class View:
    __slots__ = ("root", "ap", "lo", "hi")

    def __init__(self, root, ap, lo, hi):
        self.root = root; self.ap = ap; self.lo = lo; self.hi = hi


class Tile:
    def __init__(self, ap, shape, esz, root=None, boff=0, tracked=True):
        self.ap = ap; self.shape = list(shape); self.esz = esz
        self.root = root if root is not None else self
        self.boff = boff
        if root is None:
            self.w = []; self.r = {}; self.tracked = tracked
        st = [1] * len(shape)
        for i in range(len(shape) - 2, 0, -1):
            st[i] = st[i + 1] * shape[i + 1]
        self.st = st

    def __getitem__(self, key):
        if not isinstance(key, tuple):
            key = (key,)
        lo = 0; hi = 0
        for i in range(1, len(self.shape)):
            k = key[i] if i < len(key) else slice(None)
            if isinstance(k, int):
                a = k; b = k
            else:
                a = k.start or 0
                b = (k.stop if k.stop is not None else self.shape[i]) - 1
            lo += a * self.st[i]; hi += b * self.st[i]
        if getattr(self.root, "whole", False):
            return View(self.root, self.ap[key], 0, 1 << 20)
        return View(self.root, self.ap[key], self.boff + lo * self.esz, self.boff + (hi + 1) * self.esz)

    def raw(self, ap, lo, hi):
        return View(self.root, ap, lo, hi)

    def sub(self, off, shape, f32=False):
        n = 1
        for d in shape[1:]:
            n *= d
        if f32 and "nobitcast" in DBG:
            t = _ALLOC("bc%d" % off, [128, n], F32)
            sh = list(shape)
            base = t[:, :]
            if len(shape) > 2:
                names = "abcdefg"[:len(shape) - 1]
                kw = {names[i]: shape[i + 1] for i in range(len(names) - 1)}
                base = base.rearrange("p (%s) -> p %s" % (" ".join(names), " ".join(names)), **kw)
            return Tile(base, shape, 4, root=self.root, boff=self.boff + off * 2)
        if f32:
            base = self.ap[:, off:off + 2 * n].bitcast(F32); esz = 4
        else:
            base = self.ap[:, off:off + n]; esz = 2
        if len(shape) > 2:
            names = "abcdefg"[:len(shape) - 1]
            kw = {names[i]: shape[i + 1] for i in range(len(names) - 1)}
            base = base.rearrange("p (%s) -> p %s" % (" ".join(names), " ".join(names)), **kw)
        return Tile(base, shape, esz, root=self.root, boff=self.boff + off * 2)


class Op:
    __slots__ = ("eng", "fn", "deps", "is_dma", "key", "signal", "sigval")


class Prog:
    ENG = ("pe", "act", "dve", "pool", "sp")

    def __init__(self):
        self.ops = []
        self.dma_issued = {}
        self.dma_barrier = {}

    def _add_dep(self, deps, eng, p):
        po = self.ops[p]
        if po.is_dma:
            k = po.key
            n = self.dma_issued[k]
            deps[("dma", k)] = n
            self.dma_barrier[k] = n
        else:
            if po.eng == "pe" and eng == "pe":
                return
            kk = ("eng", po.eng)
            if deps.get(kk, -1) < p:
                deps[kk] = p

    def record(self, eng, fn, reads, writes, is_dma=False, key=None):
        i = len(self.ops)
        deps = {}
        for v in reads:
            if v is None or not v.root.tracked:
                continue
            for (lo, hi, p) in v.root.w:
                if not (hi <= v.lo or lo >= v.hi):
                    self._add_dep(deps, eng, p)
        for v in writes:
            if not v.root.tracked:
                continue
            for (lo, hi, p) in v.root.w:
                if not (hi <= v.lo or lo >= v.hi):
                    self._add_dep(deps, eng, p)
            for (lo, hi, _e), p in v.root.r.items():
                if not (hi <= v.lo or lo >= v.hi):
                    self._add_dep(deps, eng, p)
        if is_dma:
            b = self.dma_barrier.get(key, 0)
            if b > 0:
                kk = ("dma", key)
                deps[kk] = max(deps.get(kk, 0), b)
            self.dma_issued[key] = self.dma_issued.get(key, 0) + 1
        for (kind, e), p in deps.items():
            if kind == "eng":
                self.ops[p].signal = True
        o = Op()
        o.eng = eng; o.fn = fn; o.deps = deps; o.is_dma = is_dma; o.key = key; o.signal = False; o.sigval = 0
        self.ops.append(o)
        for v in writes:
            if not v.root.tracked:
                continue
            r = v.root
            r.w = [e for e in r.w if not (e[0] >= v.lo and e[1] <= v.hi)]
            r.w.append((v.lo, v.hi, i))
            dead = [k for k in r.r if k[0] >= v.lo and k[1] <= v.hi]
            for k in dead:
                del r.r[k]
        for v in reads:
            if v is None or not v.root.tracked:
                continue
            tag = ("d", i) if is_dma else eng
            v.root.r[(v.lo, v.hi, tag)] = i
        return i

    def op(self, eng, method, **kw):
        reads = []; writes = []; args = {}
        for k, v in kw.items():
            if isinstance(v, View):
                (writes if k in ("out", "accum_out", "ap") else reads).append(v)
                args[k] = v.ap
            else:
                args[k] = v

        def fn(e, method=method, args=args):
            return getattr(e, method)(**args)
        i = self.record(eng, fn, reads, writes)
        self.ops[i].fn.__dict__['desc'] = (method, {k: (str(v.shape) if hasattr(v, 'shape') else v) for k, v in args.items()})
        return i

    def dma(self, q, out, in_, key):
        args = {"out": out.ap, "in_": in_.ap}

        def fn(e, args=args):
            return e.dma_start(**args)
        return self.record(q, fn, [in_], [out], is_dma=True, key=key)

    def emit(self, nc):
        cnt = {e: 0 for e in self.ENG}
        for o in self.ops:
            if (not o.is_dma) and o.signal:
                cnt[o.eng] += 1
                o.sigval = cnt[o.eng]
        with ExitStack() as es:
            esem = {e: es.enter_context(nc.semaphore("s_" + e)) for e in self.ENG}
            dsem = {k: es.enter_context(nc.semaphore("d_" + k)) for k in self.dma_issued}
            block = es.enter_context(nc.Block())
            ops = self.ops if MAXOPS is None else self.ops[:MAXOPS]
            issued = {}
            for o in ops:
                if o.is_dma:
                    issued[o.key] = issued.get(o.key, 0) + 1

            def run(name, e):
                waited = {}
                for o in ops:
                    if o.eng != name:
                        continue
                    for (kind, k), p in o.deps.items():
                        if kind == "eng":
                            sem = esem[k]; val = ops[p].sigval; sk = "e" + k
                        else:
                            sem = dsem[k]; val = 16 * p; sk = "d" + k
                        if waited.get(sk, 0) < val:
                            e.wait_ge(sem, val)
                            waited[sk] = val
                    ins = o.fn(e)
                    if o.is_dma:
                        ins.then_inc(dsem[o.key], 16)
                    elif o.signal:
                        ins.then_inc(esem[name], 1)
                if name == "sp":
                    for k, n in issued.items():
                        e.wait_ge(dsem[k], 16 * n)

            @block.tensor
            def _(e):
                run("pe", e)

            @block.scalar
            def _(e):
                run("act", e)

            @block.vector
            def _(e):
                run("dve", e)

            @block.gpsimd
            def _(e):
                run("pool", e)

            @block.sync
            def _(e):
                run("sp", e)


WSHAPES = {"w_in": (D, NIN), "w_out": (D, D), "xq": (D, D), "xkv": (D, 2 * D), "xo": (D, D),
           "w_up": (D, 2 * DFF), "w_down": (DFF, D)}


class _Stop(Exception):
    pass


NOCONV = False
MAXOPS = None
DBG = set()
_ALLOC = None


def _wtiles():
    t = {}
    t["w_in"] = [(0, KC, c0, 512) for c0 in (0, 512, 1024, 1536, 2056, 2568, 3080, 3592, 4104, 4616, 5128, 5640)]
    for k in ("w_out", "xq", "xo"):
        t[k] = [(0, KC, wt * 512, 512) for wt in range(4)]
    t["xkv"] = [(0, KC, wt * 512, 512) for wt in range(8)]
    up = []
    for half in (0, DFF):
        for g in range(FG):
            for (o, w) in ((0, 512), (512, 512), (1024, 384)):
                up.append((0, KC, half + g * FGC * 128 + o, w))
    t["w_up"] = up
    t["w_down"] = [(g * FGC, FGC, wt * 512, 512) for g in range(FG) for wt in range(4)]
    return t


WT = _wtiles()
WIDX = {k: {(r0, c0): i for i, (r0, nk, c0, nc_) in enumerate(v)} for k, v in WT.items()}


def build(NL, NSEQ, S, STOP=None):
    NT = S // TT
    NTOK = NSEQ * S
    nc = bass.Bass("TRN2", target_bir_lowering=False)
    P = Prog()
    es = ExitStack()

    def din(name, shape, dt=F32):
        return nc.dram_tensor(name, shape, dt, kind="ExternalInput").ap()

    x_d = din("x", [NTOK, D]); mem_d = din("mem", [NSEQ * 256, D])
    cst_d = din("cst", [128, NCST]); m01_d = din("m01", [128, 640]); cpk_d = din("cpk", [NL, 128, NCPK]); btab_d = din("btab", [NL, 128, 8, 640])
    w_d = {k: din(k, [NL, r, n]) for k, (r, n) in WSHAPES.items()} if "noweights" not in DBG else {}
    out_d = nc.dram_tensor("out", [NTOK, D], F32, kind="ExternalOutput").ap()
    xs_ap = nc.dram_tensor("xs", [128, KC, NTOK], F32, kind="Internal").ap()
    scr_ap = {k: nc.dram_tensor("scr_" + k, [NL, len(WT[k]), 128, KC * 512], BF16, kind="Internal").ap() for k in WSHAPES}
    scrg_ap = nc.dram_tensor("scr_wg", [NL, 128, KC * 8], BF16, kind="Internal").ap()
    NOTRK = Tile(None, [1, 1], 4, tracked=False)
    mkv_ap = nc.dram_tensor("mkv", [128, 8192], BF16, kind="Internal").ap()
    mkv_t = Tile(mkv_ap, [128, 8192], 2)
    xs_t = Tile(xs_ap, [128, KC, NTOK], 4)
    scr_t = {(k, l, t): Tile(None, [1, 1], 2) for k in WSHAPES for l in range(NL) for t in range(len(WT[k]))}
    scrg_t = {l: Tile(None, [1, 1], 2) for l in range(NL)}

    def ro(ap):
        return View(NOTRK, ap, 0, 0)

    global _ALLOC
    _ALLOC = lambda name, shape, dt: es.enter_context(nc.sbuf_tensor("sb_" + name, shape, dt))

    def sb(name, shape, dt):
        if "nosb" in DBG and name in ("xres", "xb", "actA", "kring", "vring", "wb0", "wb1", "EB", "cext", "ext", "acc"):
            shape = [128, 2, 8]
        if "nosb2" in DBG and name not in ("cst", "arena", "identb", "onesb", "cpk", "xres", "EB"):
            return Tile(None, shape, 4 if dt == F32 else 2)
        t = es.enter_context(nc.sbuf_tensor("sb_" + name, shape, dt))
        return Tile(t, shape, 4 if dt == F32 else 2)

    xres = sb("xres", [128, KC, TT], F32)
    xb = sb("xb", [128, KC, TT], BF16)
    actA = sb("actA", [128, KC, TT], BF16)
    NA = 14400
    arena = sb("arena", [128, NA], BF16)
    qkT = arena.sub(0, [128, 8, TT]); sigmo = arena.sub(4096, [128, 4, TT]); ktok = arena.sub(6144, [128, 8, TT])
    mvaug = arena.sub(10240, [128, 8, 4, 130]); aqT = arena.sub(0, [128, 8, TT])
    cacc = arena.sub(4096, [128, 4, TT], f32=True); stage = arena.sub(8192, [128, D], f32=True)
    cext = arena.sub(8192, [128, 4, 542], f32=True)
    hg = [arena.sub(0, [128, FGC, TT]), arena.sub(FGC * TT, [128, FGC, TT])]
    memT = arena.sub(0, [128, KC, 256])
    kring = sb("kring", [128, 8, 1024 if "smallsb" not in DBG else 16], BF16); vring = sb("vring", [128, 8, 1024 if "smallsb" not in DBG else 16], BF16)
    wbuf = [sb("wb0", [128, KC, 512], BF16), sb("wb1", [128, KC, 512], BF16)]
    wg = sb("wg", [128, KC, 8], BF16)
    memKT = arena.sub(4096, [128, KC, 256]); memV = arena.sub(8192, [128, 2, D])
    EB = sb("EB", [128, 8, 640], BF16)
    cst = sb("cst", [128, NCST], F32); cpk = sb("cpk", [128, NCPK], F32)
    identb = sb("identb", [128, 128], BF16); onesb = sb("onesb", [128, 128], BF16)
    ext = sb("ext", [128, 2, 515], F32); acc = sb("acc", [128, 2, TT], F32)
    etmp = arena.sub(4096, [128, 2, 640]); PTa = arena.sub(5376, [128, 2, 640]); PTx = arena.sub(0, [128, 2, 2, TT])
    ebt = arena.sub(0, [128, 640], f32=True); m01 = arena.sub(2048, [128, 640], f32=True)
    mhalo = sb("mhalo", [128, 8, 3], F32); fhalo = sb("fhalo", [128, 44, 2], F32)
    Cst = sb("Cst", [128, 516], F32)
    Cb = sb("Cb", [128, 2, 516], BF16)
    g_ig = sb("g_ig", [64, 32], F32); g_zf = sb("g_zf", [64, 32], F32); g_l1 = sb("g_l1", [64, 32], F32)
    g_nb = sb("g_nb", [64, 32], F32); g_u = sb("g_u", [64, 32], F32); g_eu = sb("g_eu", [64, 32], F32)
    g_enb = sb("g_enb", [64, 32], F32); g_eg = sb("g_eg", [128, 32], F32)
    STm = sb("STm", [64, 2, 256], BF16)
    hn = sb("hn", [64, 4, 128], BF16)
    hn2 = sb("hn2", [128, 256], BF16); vtmp = sb("vtmp", [128, 2, TT], BF16); ltmp = sb("ltmp", [128, 2, TT], BF16); chalo = sb("chalo", [128, 4, 30], F32)
    pnS = sb("pnS", [64, 516], F32)
    hs = sb("hs", [64, 8, 4], F32)
    ps = []
    for i in range(7):
        t = es.enter_context(nc.psum_tensor("ps%d" % i, [128, 512], F32))
        ps.append(Tile(t, [128, 512], 4))
        ps[-1].whole = True

    identf = cst[:, 0:128]; onesf = cst[:, 128:256]
    onecol = cst[:, 128:129]; epscol = cst[:, 736:737]

    st = {"wi": 0, "mb": 0, "ei": 0, "ai": 0, "cp": 0}

    def mbank():
        b = ps[st["mb"] % 4]; st["mb"] += 1
        return b

    def cp(out, in_, scale=None):
        st["cp"] += 1
        if scale is not None:
            P.op("act", "activation", out=out, in_=in_, func=AF.Copy, scale=scale)
        elif st["cp"] % 2:
            P.op("act", "activation", out=out, in_=in_, func=AF.Copy)
        else:
            P.op("dve", "tensor_copy", out=out, in_=in_)

    def mm(out, lhsT, rhs, start, stop):
        P.op("pe", "matmul", out=out, lhsT=lhsT, rhs=rhs, start=start, stop=stop)

    def convert_layer(l):
        if NOCONV:
            return
        for k in WSHAPES:
            wsrc = w_d[k][l].rearrange("(kc p) n -> p kc n", p=128)
            for t, (r0, nk, c0, ncols) in enumerate(WT[k]):
                dst = scr_ap[k][l, t].rearrange("p (kc n) -> p kc n", n=512)
                h = (nk + 1) // 2
                for j, (a, b) in enumerate(((0, h), (h, nk))):
                    P.dma("pool", out=scr_t[(k, l, t)].raw(dst[:, a:b, 0:ncols], j, j + 1),
                          in_=ro(wsrc[:, r0 + a:r0 + b, c0:c0 + ncols]), key="cv_%s%d" % (k, l))
            if k == "w_in":
                P.dma("pool", out=scrg_t[l].raw(scrg_ap[l].rearrange("p (kc n) -> p kc n", n=8), 0, 1),
                      in_=ro(wsrc[:, :, 2048:2056]), key="cv_%s%d" % (k, l))

    def wload(name, l, r0, nk, c0, ncols):
        i = st["wi"] % 2; st["wi"] += 1
        buf = wbuf[i]
        t = WIDX[name][(r0, c0)]
        assert WT[name][t] == (r0, nk, c0, ncols), (name, r0, nk, c0, ncols)
        src = scr_ap[name][l, t].rearrange("p (kc n) -> p kc n", n=512)
        P.dma("sp", out=buf[:, 0:nk, :], in_=scr_t[(name, l, t)].raw(src[:, 0:nk, :], 0, BIG), key="wb%d" % i)
        return buf

    def ln_fm(gv, bv):
        pA, pB, pM = ps[4], ps[5], ps[3]
        for c in range(KC):
            zb = ltmp[:, c % 2, :]; sq = vtmp[:, c % 2, :]
            P.op("act", "activation", out=zb, in_=xres[:, c, :], func=AF.Copy)
            P.op("act", "activation", out=sq, in_=xres[:, c, :], func=AF.Square)
            mm(pA[:, :], onesb[:, :], zb, c == 0, c == KC - 1)
            mm(pB[:, :], onesb[:, :], sq, c == 0, c == KC - 1)
        P.op("act", "activation", out=pM[:, :], in_=pA[:, :], func=AF.Copy, scale=1.0 / D)
        P.op("act", "activation", out=acc[:, 0, :], in_=pA[:, :], func=AF.Square, scale=1.0 / D)
        P.op("dve", "scalar_tensor_tensor", out=pA[:, :], in0=pB[:, :], scalar=1.0 / D, in1=acc[:, 0, :],
             op0=ALU.mult, op1=ALU.subtract)
        P.op("act", "activation", out=pA[:, :], in_=pA[:, :], func=AF.Ln, bias=epscol)
        P.op("act", "activation", out=pB[:, :], in_=pA[:, :], func=AF.Exp, scale=-0.5)
        for c in range(KC):
            xc = xres[:, c, :]
            P.op("dve", "tensor_tensor", out=xc, in0=xc, in1=pM[:, :], op=ALU.subtract)
            P.op("dve", "tensor_tensor", out=xc, in0=xc, in1=pB[:, :], op=ALU.mult)
            P.op("act", "activation", out=xc, in_=xc, func=AF.Identity, scale=gv(c), bias=bv(c))
            P.op("act", "activation", out=xb[:, c, :], in_=xc, func=AF.Copy)

    def resid(cbase, first=True):
        def f(oc, p):
            xc = xres[:, cbase + oc, :]
            if first:
                P.op("dve", "scalar_tensor_tensor", out=xc, in0=xc, scalar=ALPHA, in1=p[:, :], op0=ALU.mult, op1=ALU.add)
            else:
                P.op("dve", "tensor_tensor", out=xc, in0=xc, in1=p[:, :], op=ALU.add)
        return f

    def fm_proj(W, nk, noc, rhs_fn, consume, ncols=TT):
        for oc in range(noc):
            p = mbank()
            for kc in range(nk):
                mm(p[:, 0:ncols], W[:, kc, oc * 128:(oc + 1) * 128], rhs_fn(kc), kc == 0, kc == nk - 1)
            consume(oc, p)

    xbk = lambda kc: xb[:, kc, :]

    def chk(label):
        if STOP == label:
            raise _Stop()

    def store_out(tok0):
        for blk in range(4):
            for c in range(KC):
                pt = ps[4 + c % 2]
                P.op("pe", "transpose", out=pt[:, 0:128], in_=xres[:, c, blk * 128:(blk + 1) * 128], identity=identf)
                cp(stage[:, c * 128:(c + 1) * 128], pt[:, 0:128])
            P.dma("pool", out=ro(out_d[tok0 + blk * 128:tok0 + (blk + 1) * 128, :]), in_=stage[:, :], key="ost")

    P.dma("pool", out=cst[:, :], in_=ro(cst_d[:, :]), key="cst")
    P.op("act", "activation", out=identb[:, :], in_=identf, func=AF.Copy)
    P.op("act", "activation", out=onesb[:, :], in_=onesf, func=AF.Copy)
    convert_layer(0)

    try:
        for l in range(NL):
            P.dma("pool", out=cpk[:, :], in_=ro(cpk_d[l]), key="cpk")
            P.dma("pool", out=m01[:, :], in_=ro(m01_d[:, :]), key="m01")
            for h in range(8):
                P.dma("pool", out=ebt[:, :], in_=ro(btab_d[l, :, h, :]), key="ebt")
                P.op("act", "activation", out=ebt[:, :], in_=ebt[:, :], func=AF.Exp)
                P.op("dve", "tensor_tensor", out=EB[:, h, :], in0=ebt[:, :], in1=m01[:, :], op=ALU.mult)
            col = lambda base: (lambda c: cpk[:, base + c:base + c + 1])
            chk('setup')

            for b in range(NSEQ):
                for mb in range(2):
                    P.dma("pool", out=stage[:, :], in_=ro(mem_d[b * 256 + mb * 128:b * 256 + (mb + 1) * 128, :]), key="stage")
                    for c in range(KC):
                        pt = ps[4 + c % 2]
                        P.op("pe", "transpose", out=pt[:, 0:128], in_=stage[:, c * 128:(c + 1) * 128], identity=identf)
                        cp(memT[:, c, mb * 128:(mb + 1) * 128], pt[:, 0:128])
                for wt in range(4):
                    W = wload("xkv", l, 0, KC, wt * 512, 512)
                    fm_proj(W, KC, 4, lambda kc: memT[:, kc, :],
                            lambda oc, p, wt=wt: cp(memKT[:, wt * 4 + oc, :], p[:, 0:256]), ncols=256)
                for wt in range(4):
                    W = wload("xkv", l, 0, KC, D + wt * 512, 512)
                    for mb in range(2):
                        p = mbank()
                        for kc in range(KC):
                            mm(p[:, :], memT[:, kc, mb * 128:(mb + 1) * 128], W[:, kc, :], kc == 0, kc == KC - 1)
                        cp(memV[:, mb, wt * 512:(wt + 1) * 512], p[:, :])
                P.dma("pool", out=mkv_t[:, 0:8192], in_=arena[:, 4096:12288], key="mkvst")
                chk('memkv')
                P.op("dve", "memset", ap=mhalo[:, :, :], constant=0.0)
                P.op("dve", "memset", ap=fhalo[:, :, :], constant=0.0)
                P.op("dve", "memset", ap=chalo[:, :, :], constant=0.0)
                P.op("dve", "memset", ap=Cst[:, :], constant=0.0)
                P.op("dve", "memset", ap=Cb[:, 0, :], constant=0.0)
                cbi = 0

                for t in range(NT):
                    tok0 = b * S + t * TT
                    if l == 0:
                        for blk in range(4):
                            P.dma("pool", out=stage[:, :], in_=ro(x_d[tok0 + blk * 128:tok0 + (blk + 1) * 128, :]), key="stage")
                            for c in range(KC):
                                pt = ps[4 + c % 2]
                                P.op("pe", "transpose", out=pt[:, 0:128], in_=stage[:, c * 128:(c + 1) * 128], identity=identf)
                                cp(xres[:, c, blk * 128:(blk + 1) * 128], pt[:, 0:128])
                        ln_fm(lambda c: cst[:, 704 + c:705 + c], lambda c: cst[:, 720 + c:721 + c])
                        chk('ln_in')
                    else:
                        for (a, bb) in ((0, 8), (8, 16)):
                            P.dma("pool", out=xres[:, a:bb, :], in_=xs_t.raw(xs_ap[:, a:bb, tok0:tok0 + TT], (tok0 // TT) * 2 + a // 8, (tok0 // TT) * 2 + a // 8 + 1), key="xres")
                        for c in range(KC):
                            cp(xb[:, c, :], xres[:, c, :])

                    P.dma("pool", out=wg[:, :, :], in_=scrg_t[l].raw(scrg_ap[l].rearrange("p (kc n) -> p kc n", n=8), 0, BIG), key="wg")

                    def conv_m(ch, p, outv):
                        e = st["ei"] % 2; st["ei"] += 1
                        a = st["ai"] % 2; st["ai"] += 1
                        P.op("act", "activation", out=ext[:, e, 0:3], in_=mhalo[:, ch, :], func=AF.Copy)
                        P.op("act", "activation", out=ext[:, e, 3:515], in_=p[:, :], func=AF.Copy)
                        P.op("act", "activation", out=mhalo[:, ch, :], in_=ext[:, e, 512:515], func=AF.Copy)
                        av = acc[:, a, :]
                        P.op("dve", "tensor_scalar", out=av, in0=ext[:, e, 0:512], scalar1=cpk[:, 96 + ch * 4:97 + ch * 4],
                             scalar2=cpk[:, 128 + ch:129 + ch], op0=ALU.mult, op1=ALU.add)
                        for j in range(1, 4):
                            P.op("dve", "scalar_tensor_tensor", out=av, in0=ext[:, e, j:j + 512],
                                 scalar=cpk[:, 96 + ch * 4 + j:97 + ch * 4 + j], in1=av, op0=ALU.mult, op1=ALU.add)
                        P.op("act", "activation", out=outv, in_=av, func=AF.Silu)

                    W = wload("w_in", l, 0, KC, 0, 512)
                    fm_proj(W, KC, 4, xbk, lambda oc, p: conv_m(oc, p, qkT[:, oc, :]))
                    chk('m_q')
                    W = wload("w_in", l, 0, KC, 512, 512)
                    fm_proj(W, KC, 4, xbk, lambda oc, p: conv_m(4 + oc, p, qkT[:, 4 + oc, :]))
                    chk('m_k')
                    for c in range(8):
                        pt = ps[6 if c % 2 == 0 else 3]
                        for h in range(4):
                            mm(pt[0:64, h * 128:(h + 1) * 128], qkT[:, 4 + h, c * 64:(c + 1) * 64], identb[:, :], True, True)
                        cp(ktok[0:64, c, :], pt[0:64, :], scale=128.0 ** -0.5)
                    chk('m_kt')
                    pgf = ps[4]
                    for kc in range(KC):
                        mm(pgf[0:8, :], wg[:, kc, 0:8], xb[:, kc, :], kc == 0, kc == KC - 1)
                    P.op("act", "activation", out=acc[0:8, 0, :], in_=pgf[0:8, :], func=AF.Copy)
                    pg = ps[5]
                    for c in range(8):
                        mm(pg[0:64, c * 4:(c + 1) * 4], acc[0:8, 0, c * 64:(c + 1) * 64], cst[0:8, 0:4], True, True)
                        mm(pg[0:64, 32 + c * 4:32 + (c + 1) * 4], acc[0:8, 0, c * 64:(c + 1) * 64], cst[0:8, 4:8], True, True)
                    P.op("dve", "tensor_tensor", out=g_ig[:, :], in0=pg[0:64, 0:32], in1=cpk[0:64, 452:484], op=ALU.add)
                    P.op("dve", "tensor_tensor", out=g_zf[:, :], in0=pg[0:64, 32:64], in1=cpk[0:64, 484:516], op=ALU.add)
                    P.op("act", "activation", out=g_zf[:, :], in_=g_zf[:, :], func=AF.Exp, scale=-1.0)
                    P.op("act", "activation", out=g_l1[:, :], in_=g_zf[:, :], func=AF.Ln, bias=cst[0:64, 128:129])
                    mm(ps[4][0:64, 0:32], cst[0:64, 256:320], g_l1[:, :], True, True)
                    P.op("dve", "tensor_copy", out=g_nb[:, :], in_=ps[4][0:64, 0:32])
                    P.op("dve", "tensor_tensor", out=g_u[:, :], in0=g_ig[:, :], in1=g_nb[:, :], op=ALU.add)
                    P.op("act", "activation", out=g_eu[:, :], in_=g_u[:, :], func=AF.Exp)
                    P.op("act", "activation", out=g_enb[:, :], in_=g_nb[:, :], func=AF.Exp)
                    mm(ps[4][:, 32:64], cst[0:64, 576:704], g_nb[:, :], True, True)
                    P.op("act", "activation", out=g_eg[:, :], in_=ps[4][:, 32:64], func=AF.Exp, scale=-1.0)
                    chk('m_g')
                    W = wload("w_in", l, 0, KC, 1024, 512)
                    for c in range(8):
                        p = ps[3]
                        for kc in range(KC):
                            mm(p[0:64, :], xb[:, kc, c * 64:(c + 1) * 64], W[:, kc, :], kc == 0, kc == KC - 1)
                        for h in range(4):
                            P.op("dve", "tensor_scalar", out=mvaug[0:64, c, h, 0:128], in0=p[0:64, h * 128:(h + 1) * 128],
                                 scalar1=g_eu[:, c * 4 + h:c * 4 + h + 1], scalar2=None, op0=ALU.mult)
                        P.op("act", "activation", out=mvaug[0:64, c, :, 128], in_=g_eu[:, c * 4:c * 4 + 4], func=AF.Copy)
                    chk('m_v')
                    W = wload("w_in", l, 0, KC, 1536, 512)
                    fm_proj(W, KC, 4, xbk, lambda oc, p: P.op("act", "activation", out=sigmo[:, oc, :], in_=p[:, :], func=AF.Sigmoid))
                    chk('m_o')
                    for c in range(8):
                        par = c % 2
                        cs = slice(c * 64, (c + 1) * 64)
                        pkq = ps[4]
                        for h in range(4):
                            mm(pkq[0:64, par * 256 + h * 64:par * 256 + (h + 1) * 64], qkT[:, 4 + h, cs], qkT[:, h, cs], True, True)
                        P.op("dve", "tensor_tensor", out=STm[:, par, :], in0=pkq[0:64, par * 256:(par + 1) * 256],
                             in1=cst[0:64, 320:576], op=ALU.mult)
                        def pnv(h, a, b):
                            return (ps[5] if h < 2 else ps[3])[0:64, (h % 2) * 129 + a:(h % 2) * 129 + b]

                        def pkvv(h):
                            return (ps[1] if h < 2 else ps[2])[:, (h % 2) * 129:(h % 2) * 129 + 129]
                        for h in range(4):
                            mm(pnv(h, 0, 129), STm[:, par, h * 64:(h + 1) * 64], mvaug[0:64, c, h, 0:129], True, False)
                            mm(pnv(h, 0, 129), qkT[:, h, cs], Cb[:, cbi, h * 129:(h + 1) * 129], False, True)
                        for h in range(4):
                            mm(pkvv(h), ktok[0:64, c, h * 128:(h + 1) * 128], mvaug[0:64, c, h, 0:129], True, True)
                        P.op("act", "activation", out=pnS[:, 0:258], in_=ps[5][0:64, 0:258], func=AF.Copy)
                        P.op("dve", "tensor_copy", out=pnS[:, 258:516], in_=ps[3][0:64, 0:258])
                        denv = pnS.raw(pnS.ap[:, :].rearrange("p (h e) -> p h e", h=4)[:, :, 128], 0, 2064)
                        P.op("dve", "scalar_tensor_tensor", out=hs[:, 0, :], in0=denv, scalar=-1.0, in1=denv, op0=ALU.mult, op1=ALU.max)
                        P.op("dve", "tensor_tensor", out=hs[:, 0, :], in0=hs[:, 0, :], in1=g_enb[:, c * 4:c * 4 + 4], op=ALU.max)
                        P.op("dve", "reciprocal", out=hs[:, 0, :], in_=hs[:, 0, :])
                        for h in range(4):
                            P.op("dve", "tensor_scalar", out=acc[0:64, 1, h * 128:(h + 1) * 128], in0=pnS[:, h * 129:h * 129 + 128],
                                 scalar1=hs[:, 0, h:h + 1], scalar2=None, op0=ALU.mult)
                        P.op("dve", "tensor_reduce", out=hs[:, 1, :], in_=acc.raw(acc.ap[0:64, 1, :].rearrange("p (h e) -> p h e", h=4), 2048, 4096), axis=AX.X, op=ALU.add)
                        P.op("act", "activation", out=acc[0:64, 0, :], in_=acc[0:64, 1, :], func=AF.Square)
                        P.op("dve", "tensor_reduce", out=hs[:, 2, :], in_=acc.raw(acc.ap[0:64, 0, :].rearrange("p (h e) -> p h e", h=4), 0, 2048), axis=AX.X, op=ALU.add)
                        P.op("dve", "tensor_scalar", out=hs[:, 3, :], in0=hs[:, 1, :], scalar1=1.0 / 128, scalar2=None, op0=ALU.mult)
                        P.op("dve", "tensor_tensor", out=hs[:, 6, :], in0=hs[:, 3, :], in1=hs[:, 3, :], op=ALU.mult)
                        P.op("dve", "scalar_tensor_tensor", out=hs[:, 4, :], in0=hs[:, 2, :], scalar=1.0 / 128, in1=hs[:, 6, :],
                             op0=ALU.mult, op1=ALU.subtract)
                        P.op("act", "activation", out=hs[:, 4, :], in_=hs[:, 4, :], func=AF.Ln, bias=cst[0:64, 736:737])
                        P.op("act", "activation", out=hs[:, 5, :], in_=hs[:, 4, :], func=AF.Exp, scale=-0.5)
                        for h in range(4):
                            P.op("dve", "tensor_scalar", out=hn[:, h, :], in0=acc[0:64, 1, h * 128:(h + 1) * 128], scalar1=hs[:, 3, h:h + 1],
                                 scalar2=hs[:, 5, h:h + 1], op0=ALU.subtract, op1=ALU.mult)
                        half = par * 256
                        for h in range(4):
                            mm(ps[6][:, half + h * 64:half + (h + 1) * 64], hn[:, h, :], identb[0:64, 0:64], True, True)
                        for h in range(4):
                            P.op("act", "activation", out=hn2[:, h * 64:(h + 1) * 64], in_=ps[6][:, half + h * 64:half + (h + 1) * 64],
                                 func=AF.Identity, scale=cpk[:, 136 + h:137 + h])
                            P.op("dve", "tensor_tensor", out=actA[:, h, cs], in0=hn2[:, h * 64:(h + 1) * 64], in1=sigmo[:, h, cs], op=ALU.mult)
                        P.op("dve", "tensor_tensor", out=Cst[:, 0:258], in0=Cst[:, 0:258], in1=ps[1][:, 0:258], op=ALU.add)
                        P.op("dve", "tensor_tensor", out=Cst[:, 258:516], in0=Cst[:, 258:516], in1=ps[2][:, 0:258], op=ALU.add)
                        for h in range(4):
                            P.op("dve", "tensor_scalar", out=Cst[:, h * 129:(h + 1) * 129], in0=Cst[:, h * 129:(h + 1) * 129],
                                 scalar1=g_eg[:, c * 4 + h:c * 4 + h + 1], scalar2=None, op0=ALU.mult)
                        cbi ^= 1
                        P.op("act", "activation", out=Cb[:, cbi, :], in_=Cst[:, :], func=AF.Copy)
                        chk('m_c%d' % c)

                    chk('mlstm')
                    W = wload("w_in", l, 0, KC, 2056, 512)
                    fm_proj(W, KC, 4, xbk, lambda oc, p: cp(cacc[:, oc, :], p[:, :]))
                    W = wload("w_in", l, 0, KC, 2568, 512)

                    def glu(oc, p):
                        a = st["ai"] % 2; st["ai"] += 1
                        P.op("act", "activation", out=acc[:, a, :], in_=p[:, :], func=AF.Sigmoid)
                        P.op("dve", "tensor_tensor", out=cext[:, oc, 30:542], in0=cacc[:, oc, :], in1=acc[:, a, :], op=ALU.mult)
                    fm_proj(W, KC, 4, xbk, glu)
                    for ch in range(4):
                        cv = cacc[:, ch, :]
                        P.op("act", "activation", out=cext[:, ch, 0:30], in_=chalo[:, ch, :], func=AF.Copy)
                        P.op("dve", "tensor_scalar", out=cv, in0=cext[:, ch, 0:512], scalar1=cpk[:, 140 + ch * 31:141 + ch * 31],
                             scalar2=cpk[:, 264 + ch:265 + ch], op0=ALU.mult, op1=ALU.add)
                        for j in range(1, 31):
                            P.op("dve", "scalar_tensor_tensor", out=cv, in0=cext[:, ch, j:j + 512],
                                 scalar=cpk[:, 140 + ch * 31 + j:141 + ch * 31 + j], in1=cv, op0=ALU.mult, op1=ALU.add)
                        P.op("act", "activation", out=chalo[:, ch, :], in_=cext[:, ch, 512:542], func=AF.Copy)
                    pA, pB, pM = ps[4], ps[5], ps[3]
                    for ch in range(4):
                        sq = acc[:, ch % 2, :]
                        P.op("act", "activation", out=sq, in_=cacc[:, ch, :], func=AF.Square)
                        mm(pA[:, :], onesf, cacc[:, ch, :], ch == 0, ch == 3)
                        mm(pB[:, :], onesf, sq, ch == 0, ch == 3)
                    P.op("act", "activation", out=pM[:, :], in_=pA[:, :], func=AF.Copy, scale=1.0 / 512)
                    P.op("act", "activation", out=acc[:, 0, :], in_=pA[:, :], func=AF.Square, scale=1.0 / 512)
                    P.op("dve", "scalar_tensor_tensor", out=pA[:, :], in0=pB[:, :], scalar=1.0 / 512, in1=acc[:, 0, :],
                         op0=ALU.mult, op1=ALU.subtract)
                    P.op("act", "activation", out=pA[:, :], in_=pA[:, :], func=AF.Ln, bias=epscol)
                    P.op("act", "activation", out=pB[:, :], in_=pA[:, :], func=AF.Exp, scale=-0.5)
                    for ch in range(4):
                        cv = cacc[:, ch, :]
                        P.op("dve", "tensor_tensor", out=cv, in0=cv, in1=pM[:, :], op=ALU.subtract)
                        P.op("dve", "tensor_tensor", out=cv, in0=cv, in1=pB[:, :], op=ALU.mult)
                        P.op("act", "activation", out=actA[:, 4 + ch, :], in_=cv, func=AF.Silu,
                             scale=cpk[:, 268 + ch:269 + ch], bias=cpk[:, 272 + ch:273 + ch])

                    chk('cconv')
                    for wt in range(2):
                        W = wload("w_in", l, 0, KC, 3080 + wt * 512, 512)
                        fm_proj(W, KC, 4, xbk, lambda oc, p, wt=wt: cp(aqT[:, wt * 4 + oc, :], p[:, :], scale=128.0 ** -0.5))
                    slot0 = (t * 4) % 8
                    for wt in range(2):
                        W = wload("w_in", l, 0, KC, 4104 + wt * 512, 512)
                        fm_proj(W, KC, 4, xbk, lambda oc, p, wt=wt: cp(kring[:, wt * 4 + oc, slot0 * 128:slot0 * 128 + 512], p[:, :]))
                    for wt in range(2):
                        W = wload("w_in", l, 0, KC, 5128 + wt * 512, 512)
                        for blk in range(4):
                            p = mbank()
                            for kc in range(KC):
                                mm(p[:, :], xb[:, kc, blk * 128:(blk + 1) * 128], W[:, kc, :], kc == 0, kc == KC - 1)
                            cp(vring[:, slot0 + blk, wt * 512:(wt + 1) * 512], p[:, :])
                    sk = 0
                    for h in range(8):
                        po = ps[0]; pdn = ps[1]
                        for j in range(4):
                            gq = t * 4 + j
                            ivs = [i for i in range(5) if gq - 4 + i >= 0]
                            k2 = sk % 2; sk += 1
                            pa = ps[4 if k2 == 0 else 6]; pb5 = ps[5 if k2 == 0 else 3]
                            qv = aqT[:, h, j * 128:(j + 1) * 128]

                            def sview(i):
                                return pa[:, i * 128:(i + 1) * 128] if i < 4 else pb5[:, 0:128]
                            for i in ivs:
                                sl = (gq - 4 + i) % 8
                                mm(sview(i), kring[:, h, sl * 128:(sl + 1) * 128], qv, True, True)
                            i0 = ivs[0]
                            if i0 < 4:
                                P.op("act", "activation", out=etmp[:, k2, i0 * 128:512], in_=pa[:, i0 * 128:512], func=AF.Exp)
                            P.op("act", "activation", out=etmp[:, k2, 512:640], in_=pb5[:, 0:128], func=AF.Exp)
                            P.op("dve", "tensor_tensor", out=PTa[:, k2, i0 * 128:640], in0=etmp[:, k2, i0 * 128:640],
                                 in1=EB[:, h, i0 * 128:640], op=ALU.mult)
                            for n, i in enumerate(ivs):
                                sl = (gq - 4 + i) % 8
                                mm(po[:, j * 128:(j + 1) * 128], vring[:, sl, h * 128:(h + 1) * 128], PTa[:, k2, i * 128:(i + 1) * 128],
                                   n == 0, n == len(ivs) - 1)
                            for n, i in enumerate(ivs):
                                mm(pdn[:, j * 128:(j + 1) * 128], onesb[:, :], PTa[:, k2, i * 128:(i + 1) * 128], n == 0, n == len(ivs) - 1)
                        P.op("dve", "reciprocal", out=acc[:, 1, :], in_=pdn[:, :])
                        P.op("dve", "tensor_tensor", out=actA[:, 8 + h, :], in0=po[:, :], in1=acc[:, 1, :], op=ALU.mult)

                    chk('attn')
                    for wt in range(4):
                        W = wload("w_out", l, 0, KC, wt * 512, 512)
                        fm_proj(W, KC, 4, lambda kc: actA[:, kc, :], resid(wt * 4))
                    ln_fm(col(0), col(16))

                    chk('ln1')
                    P.dma("pool", out=arena[:, 4096:12288], in_=mkv_t[:, 0:8192], key="mkv")
                    for wt in range(4):
                        W = wload("xq", l, 0, KC, wt * 512, 512)
                        fm_proj(W, KC, 4, xbk, lambda oc, p, wt=wt: cp(actA[:, wt * 4 + oc, :], p[:, :], scale=512.0 ** -0.5))
                    for hx in range(4):
                        k2 = hx % 2
                        for mb in range(2):
                            p = ps[4 + mb]
                            for dc in range(4):
                                mm(p[:, :], memKT[:, hx * 4 + dc, mb * 128:(mb + 1) * 128], actA[:, hx * 4 + dc, :], dc == 0, dc == 3)
                            P.op("act", "activation", out=PTx[:, k2, mb, :], in_=p[:, :], func=AF.Exp)
                        pdn = ps[3]
                        for mb in range(2):
                            mm(pdn[:, :], onesb[:, :], PTx[:, k2, mb, :], mb == 0, mb == 1)
                        P.op("dve", "reciprocal", out=acc[:, 1, :], in_=pdn[:, :])
                        for fc in range(4):
                            p = mbank()
                            for mb in range(2):
                                mm(p[:, :], memV[:, mb, (hx * 4 + fc) * 128:(hx * 4 + fc + 1) * 128], PTx[:, k2, mb, :], mb == 0, mb == 1)
                            P.op("dve", "tensor_tensor", out=xb[:, hx * 4 + fc, :], in0=p[:, :], in1=acc[:, 1, :], op=ALU.mult)
                    for wt in range(4):
                        W = wload("xo", l, 0, KC, wt * 512, 512)
                        fm_proj(W, KC, 4, xbk, resid(wt * 4))
                    ln_fm(col(32), col(48))

                    chk('ln2')
                    for g in range(FG):
                        hgv = hg[g % 2]
                        widths = (512, 512, 384)
                        c0 = g * FGC * 128
                        jj = 0
                        for wcols in widths:
                            W = wload("w_up", l, 0, KC, c0 + jj * 128, wcols)

                            def gate_c(oc, p, jj=jj):
                                j = g * FGC + jj + oc
                                e = st["ei"] % 2; st["ei"] += 1
                                a = st["ai"] % 2; st["ai"] += 1
                                P.op("act", "activation", out=ext[:, e, 0:2], in_=fhalo[:, j, :], func=AF.Copy)
                                P.op("act", "activation", out=ext[:, e, 2:514], in_=p[:, :], func=AF.Copy)
                                P.op("act", "activation", out=fhalo[:, j, :], in_=ext[:, e, 512:514], func=AF.Copy)
                                av = acc[:, a, :]
                                P.op("dve", "tensor_scalar", out=av, in0=ext[:, e, 0:512], scalar1=cpk[:, 276 + j * 3:277 + j * 3],
                                     scalar2=cpk[:, 408 + j:409 + j], op0=ALU.mult, op1=ALU.add)
                                for q in range(1, 3):
                                    P.op("dve", "scalar_tensor_tensor", out=av, in0=ext[:, e, q:q + 512],
                                         scalar=cpk[:, 276 + j * 3 + q:277 + j * 3 + q], in1=av, op0=ALU.mult, op1=ALU.add)
                                P.op("act", "activation", out=hgv[:, jj + oc, :], in_=av, func=AF.Gelu)
                            fm_proj(W, KC, wcols // 128, xbk, gate_c)
                            jj += wcols // 128
                        jj = 0
                        for wcols in widths:
                            W = wload("w_up", l, 0, KC, DFF + c0 + jj * 128, wcols)
                            def val_c(oc, p, jj=jj):
                                a = st["ai"] % 2; st["ai"] += 1
                                P.op("act", "activation", out=vtmp[:, a, :], in_=p[:, :], func=AF.Copy)
                                P.op("dve", "tensor_tensor", out=hgv[:, jj + oc, :], in0=vtmp[:, a, :], in1=hgv[:, jj + oc, :], op=ALU.mult)
                            fm_proj(W, KC, wcols // 128, xbk, val_c)
                            jj += wcols // 128
                        for wt in range(4):
                            W = wload("w_down", l, g * FGC, FGC, wt * 512, 512)
                            fm_proj(W, FGC, 4, lambda kc: hgv[:, kc, :], resid(wt * 4, first=(g == 0)))
                    ln_fm(col(64), col(80))

                    if b == 0 and t == 0 and l + 1 < NL:
                        convert_layer(l + 1)
                    if l == NL - 1:
                        store_out(tok0)
                    else:
                        for (a, bb) in ((0, 8), (8, 16)):
                            P.dma("pool", out=xs_t.raw(xs_ap[:, a:bb, tok0:tok0 + TT], (tok0 // TT) * 2 + a // 8, (tok0 // TT) * 2 + a // 8 + 1), in_=xres[:, a:bb, :], key="xst")

    except _Stop:
        if STOP == 'm_g':
            P.op('dve', 'tensor_copy', out=xres[0:64, 0, 0:32], in_=g_eu[:, :])
            P.op('dve', 'tensor_copy', out=xres[0:64, 0, 32:64], in_=g_enb[:, :])
            P.op('dve', 'tensor_copy', out=xres[:, 0, 64:96], in_=g_eg[:, :])
            P.op('dve', 'tensor_copy', out=xres[0:64, 0, 96:128], in_=g_nb[:, :])
        if STOP == 'm_k':
            for c in range(8):
                cp(xres[:, c, :], qkT[:, c, :])
        if STOP in ('mlstm', 'cconv', 'attn'):
            for c in range(KC):
                cp(xres[:, c, :], actA[:, c, :])
        store_out(0)
    P.emit(nc)
    es.close()
    return nc


def _consts():
    c = np.zeros((128, NCST), np.float32)
    c[:, 0:128] = np.eye(128, dtype=np.float32)
    c[:, 128:256] = 1.0
    tri = np.triu(np.ones((64, 64), np.float32))
    c[0:64, 256:320] = tri
    c[0:64, 320:576] = np.tile(tri * np.float32(128.0 ** -0.5), (1, 4))
    c[63, 576:704] = 1.0
    m = np.ones((128, 5, 128), np.float32)
    m[64:, 4, :64] = 0.0
    m[:64, 0, 64:] = 0.0
    c[:, 736] = EPS
    return c, np.ascontiguousarray(m.reshape(128, 640))


def _colmajor(v):
    return np.ascontiguousarray(np.asarray(v, np.float32).reshape(-1, 128).T)


def _prep(inp, NL):
    cst, m01 = _consts()
    cst[:, 704:720] = _colmajor(inp["ln_in_g"]); cst[:, 720:736] = _colmajor(inp["ln_in_b"])
    cpk = np.zeros((NL, 128, NCPK), np.float32)
    r = np.arange(128)[:, None, None]; ii = np.arange(5)[None, :, None]; cq = np.arange(128)[None, None, :]
    dist = 128 * (4 - ii) + cq - r
    idx = np.clip(dist, -256, 256) + 256
    btab = np.zeros((NL, 128, 8, 640), np.float32)
    for l in range(NL):
        for k, off in (("ln1_g", 0), ("ln1_b", 16), ("ln2_g", 32), ("ln2_b", 48), ("ln3_g", 64), ("ln3_b", 80)):
            cpk[l, :, off:off + 16] = _colmajor(inp[k][l])
        w = np.asarray(inp["mlstm_conv_w"][l], np.float32)
        cpk[l, :, 96:128] = w.reshape(4, 8, 128).transpose(2, 1, 0).reshape(128, 32)
        cpk[l, :, 128:136] = _colmajor(inp["mlstm_conv_b"][l])
        cpk[l, :, 136:140] = _colmajor(inp["mlstm_norm_g"][l])
        w = np.asarray(inp["cconv_w"][l], np.float32)
        cpk[l, :, 140:264] = w.reshape(31, 4, 128).transpose(2, 1, 0).reshape(128, 124)
        cpk[l, :, 264:268] = _colmajor(inp["cconv_b"][l])
        cpk[l, :, 268:272] = _colmajor(inp["cconv_ln_g"][l])
        cpk[l, :, 272:276] = _colmajor(inp["cconv_ln_b"][l])
        w = np.asarray(inp["ffn_conv_w"][l], np.float32)
        cpk[l, :, 276:408] = w.reshape(3, 44, 128).transpose(2, 1, 0).reshape(128, 132)
        cpk[l, :, 408:452] = _colmajor(inp["ffn_conv_b"][l])
        cpk[l, :, 452:484] = np.tile(np.asarray(inp["mlstm_ig_b"][l], np.float32), 8)[None, :]
        cpk[l, :, 484:516] = np.tile(np.asarray(inp["mlstm_fg_b"][l], np.float32), 8)[None, :]
        rb = np.asarray(inp["rel_bias"][l], np.float32)
        btab[l] = rb[:, idx].transpose(1, 0, 2, 3).reshape(128, 8, 640)
    return cst, m01, cpk, btab


_NC_CACHE = {}


def run_model(inp, NL, NSEQ, S, ncores):
    key = (NL, NSEQ, S)
    if key not in _NC_CACHE:
        _NC_CACHE[key] = build(NL, NSEQ, S)
    nc = _NC_CACHE[key]
    cst, m01, cpk, btab = _prep(inp, NL)
    x = np.asarray(inp["x"], np.float32); mem = np.asarray(inp["mem"], np.float32)
    wts = {k: np.ascontiguousarray(np.asarray(inp["ffn_" + k if k in ("w_up", "w_down") else k], np.float32)[:NL])
           for k in WSHAPES}
    in_maps = []
    for i in range(ncores):
        m = {"x": np.ascontiguousarray(x[i * NSEQ:(i + 1) * NSEQ].reshape(NSEQ * S, D)),
             "mem": np.ascontiguousarray(mem[i * NSEQ:(i + 1) * NSEQ].reshape(NSEQ * 256, D)),
             "cst": cst, "m01": m01, "cpk": cpk, "btab": btab}
        if "noweights" not in DBG:
            m.update(wts)
        in_maps.append(m)
    res = run_bass_kernel_spmd(nc, in_maps, core_ids=list(range(ncores)))
    outs = [np.asarray(r["out"]).reshape(NSEQ, S, D) for r in res.results]
    return np.concatenate(outs, axis=0).astype(np.float32)


def kernel(**inputs):
    return run_model(inputs, 4, 2, 2048, 8)
```
